# Optimizing a Trainium2 kernel written in Bass

```python
import math
import jax
import jax.numpy as jnp
from jax import lax
import numpy as np

D_MODEL = 1024
BATCH = 16
SEQ = 2048
DEPTH = 4

PLE_DIM = 256
EPS = 1e-6
N_BRANCH = 3
GM_GROUPS = 4
GM_GROUP_CH = 128
GM_CHUNK = 128
GM_WIDTH = GM_GROUPS * GM_GROUP_CH
DA_HEADS = 4
DA_HEAD_DIM = 64
DA_V_DIM = 2 * DA_HEAD_DIM
DA_QK_WIDTH = DA_HEADS * 2 * DA_HEAD_DIM
DA_WIDTH = DA_HEADS * DA_V_DIM
ROPE_THETA = 500000.0
ROPE_DIM = DA_HEAD_DIM // 4
Q_BLOCK = 128
DN_HEADS = 4
DN_HEAD_DIM = 128
DN_WIDTH = DN_HEADS * DN_HEAD_DIM
DN_CONV = 4
DN_CHUNK = 64
SPLIT_SIZES = (GM_WIDTH, GM_WIDTH, GM_WIDTH, DA_QK_WIDTH, DA_QK_WIDTH, DA_WIDTH, DA_WIDTH, 3 * DN_WIDTH, DN_HEADS, DN_HEADS, DN_WIDTH, N_BRANCH * D_MODEL)
N_IN = sum(SPLIT_SIZES)

kernel_name = 'hybrid_gmlp_diffattn_gdn_block'


def rms_norm(x, g):
    xf = x.astype(jnp.float32)
    y = xf * lax.rsqrt(jnp.mean(xf * xf, axis=-1, keepdims=True) + EPS)
    return (y * g.astype(jnp.float32)).astype(x.dtype)


def layer_norm(x, g, b):
    xf = x.astype(jnp.float32)
    mu = jnp.mean(xf, axis=-1, keepdims=True)
    xc = xf - mu
    y = xc * lax.rsqrt(jnp.mean(xc * xc, axis=-1, keepdims=True) + EPS)
    return (y * g.astype(jnp.float32) + b.astype(jnp.float32)).astype(x.dtype)


def l2_norm(x):
    return x * lax.rsqrt(jnp.sum(x * x, axis=-1, keepdims=True) + EPS)


def partial_rotary(x, cos, sin):
    half = ROPE_DIM // 2
    xf = x.astype(jnp.float32)
    x1 = xf[..., :half]
    x2 = xf[..., half:ROPE_DIM]
    out = jnp.concatenate([x1 * cos - x2 * sin, x2 * cos + x1 * sin, xf[..., ROPE_DIM:]], axis=-1)
    return out.astype(x.dtype)


def causal_dwconv(x, w):
    k = w.shape[0]
    return lax.conv_general_dilated(x, w[:, None, :].astype(x.dtype), window_strides=(1,), padding=[(k - 1, 0)], dimension_numbers=('NWC', 'WIO', 'NWC'), feature_group_count=x.shape[-1])


def chunked_gmlp(u, v, ln_g, ln_b, w_s, b_s):
    bsz, s, _ = v.shape
    n = s // GM_CHUNK
    u = jax.nn.gelu(u)
    v = layer_norm(jax.nn.gelu(v), ln_g, ln_b)
    tri = jnp.tril(jnp.ones((GM_CHUNK, GM_CHUNK), w_s.dtype))
    vr = v.reshape(bsz, n, GM_CHUNK, GM_GROUPS, GM_GROUP_CH)
    mix = jnp.einsum('gts,bnsgc->bntgc', w_s * tri, vr) + b_s.T[:, :, None]
    return u * mix.reshape(bsz, s, GM_WIDTH)


def diff_attention(q, k, v, positions, lam_q1, lam_k1, lam_q2, lam_k2, subln_g, lambda_init):
    bsz, s, _ = q.shape
    q = q.reshape(bsz, s, DA_HEADS, 2, DA_HEAD_DIM).transpose(0, 2, 3, 1, 4)
    k = k.reshape(bsz, s, DA_HEADS, 2, DA_HEAD_DIM).transpose(0, 2, 3, 1, 4)
    v = v.reshape(bsz, s, DA_HEADS, DA_V_DIM).transpose(0, 2, 1, 3)
    inv_freq = ROPE_THETA ** (-jnp.arange(0, ROPE_DIM, 2, dtype=jnp.float32) / ROPE_DIM)
    ang = positions.astype(jnp.float32)[..., None] * inv_freq
    cos = jnp.cos(ang)[:, None, None]
    sin = jnp.sin(ang)[:, None, None]
    q = partial_rotary(q, cos, sin)
    k = partial_rotary(k, cos, sin)
    f32 = jnp.float32
    lam = jnp.exp(jnp.sum(lam_q1.astype(f32) * lam_k1.astype(f32))) - jnp.exp(jnp.sum(lam_q2.astype(f32) * lam_k2.astype(f32))) + lambda_init
    scale = DA_HEAD_DIM ** -0.5
    outs = []
    for blk in range(s // Q_BLOCK):
        lo, hi = blk * Q_BLOCK, (blk + 1) * Q_BLOCK
        sc = jnp.einsum('bhcqd,bhckd->bhcqk', q[:, :, :, lo:hi], k[:, :, :, :hi]).astype(f32) * scale
        mask = (lo + jnp.arange(Q_BLOCK))[:, None] >= jnp.arange(hi)[None, :]
        a = jax.nn.softmax(jnp.where(mask, sc, -jnp.inf), axis=-1)
        w = a[:, :, 0] - lam * a[:, :, 1]
        outs.append(jnp.einsum('bhqk,bhkd->bhqd', w.astype(v.dtype), v[:, :, :hi]))
    o = jnp.concatenate(outs, axis=2)
    o = rms_norm(o, subln_g) * (1.0 - lambda_init)
    return o.transpose(0, 2, 1, 3).reshape(bsz, s, DA_WIDTH)


def chunk_gated_delta_rule(q, k, v, g, beta):
    bsz, h, s, dk = q.shape
    dv = v.shape[-1]
    n, c = s // DN_CHUNK, DN_CHUNK
    rs = lambda t: t.reshape(bsz, h, n, c, *t.shape[3:])
    q, k, v, g, beta = rs(q), rs(k), rs(v), rs(g), rs(beta)
    g = jnp.cumsum(g, axis=-1)
    idx = jnp.arange(c)
    causal = idx[:, None] >= idx[None, :]
    strict = idx[:, None] > idx[None, :]
    decay = jnp.exp(jnp.where(causal, g[..., :, None] - g[..., None, :], -jnp.inf))
    kk = jnp.einsum('bhncd,bhnsd->bhncs', k, k)
    a_strict = jnp.where(strict, beta[..., :, None] * kk * decay, 0.0)
    rhs = jnp.concatenate([v * beta[..., None], k * (beta * jnp.exp(g))[..., None]], axis=-1)
    sol = lax.linalg.triangular_solve(a_strict, rhs, left_side=True, lower=True, unit_diagonal=True)
    u, w = sol[..., :dv], sol[..., dv:]
    qk = jnp.einsum('bhncd,bhnsd->bhncs', q, k) * decay
    g_last = g[..., -1]
    q_dec = q * jnp.exp(g)[..., None]
    k_dec = k * jnp.exp(g_last[..., None] - g)[..., None]

    def step(state, xs):
        u_i, w_i, qk_i, qd_i, kd_i, gl_i = xs
        v_new = u_i - jnp.einsum('bhck,bhkv->bhcv', w_i, state)
        o_i = jnp.einsum('bhck,bhkv->bhcv', qd_i, state) + jnp.einsum('bhcs,bhsv->bhcv', qk_i, v_new)
        state = state * jnp.exp(gl_i)[..., None, None] + jnp.einsum('bhck,bhcv->bhkv', kd_i, v_new)
        return state, o_i

    xs = tuple(jnp.moveaxis(t, 2, 0) for t in (u, w, qk, q_dec, k_dec, g_last))
    state0 = jnp.zeros((bsz, h, dk, dv), jnp.float32)
    _, o = lax.scan(step, state0, xs)
    return jnp.moveaxis(o, 0, 2).reshape(bsz, h, s, dv)


def gated_deltanet(qkv, a, b, conv_w, a_log, dt_bias, norm_g):
    bsz, s, _ = qkv.shape
    f32 = jnp.float32
    qkv = jax.nn.silu(causal_dwconv(qkv, conv_w)).astype(f32)
    q, k, v = jnp.split(qkv, 3, axis=-1)
    heads = lambda t: t.reshape(bsz, s, DN_HEADS, DN_HEAD_DIM).transpose(0, 2, 1, 3)
    q = l2_norm(heads(q)) * (DN_HEAD_DIM ** -0.5)
    k = l2_norm(heads(k))
    v = heads(v)
    g = -jnp.exp(a_log.astype(f32)) * jax.nn.softplus(a.astype(f32) + dt_bias.astype(f32))
    beta = jax.nn.sigmoid(b.astype(f32))
    o = chunk_gated_delta_rule(q, k, v, g.transpose(0, 2, 1), beta.transpose(0, 2, 1))
    o = rms_norm(o, norm_g)
    return o.transpose(0, 2, 1, 3).reshape(bsz, s, DN_WIDTH)


def setup_inputs(seed: int = 0) -> dict:
    key = jax.random.key(seed)
    ks = jax.random.split(key, 32)
    f32 = jnp.float32

    def nrm(k, shape, scale):
        return jax.random.normal(k, shape, f32) * scale

    def gain(k, shape):
        return 1.0 + 0.02 * jax.random.normal(k, shape, f32)

    x = nrm(ks[0], (BATCH, SEQ, D_MODEL), 1.0)
    p = nrm(ks[1], (DEPTH, BATCH, SEQ, PLE_DIM), 1.0)
    offsets = jax.random.randint(ks[2], (BATCH, 1), 0, 4096, dtype=jnp.int32)
    positions = offsets + jnp.arange(SEQ, dtype=jnp.int32)[None, :]
    return {
        'x': x,
        'p': p,
        'positions': positions,
        'norm_g': gain(ks[3], (DEPTH, D_MODEL)),
        'w_in': nrm(ks[4], (DEPTH, D_MODEL, N_IN), D_MODEL ** -0.5),
        'gm_ln_g': gain(ks[5], (DEPTH, GM_WIDTH)),
        'gm_ln_b': nrm(ks[6], (DEPTH, GM_WIDTH), 0.02),
        'gm_ws': nrm(ks[7], (DEPTH, GM_GROUPS, GM_CHUNK, GM_CHUNK), GM_CHUNK ** -0.5),
        'gm_bs': 1.0 + nrm(ks[8], (DEPTH, GM_GROUPS, GM_CHUNK), 0.1),
        'da_lq1': nrm(ks[9], (DEPTH, DA_HEAD_DIM), 0.1),
        'da_lk1': nrm(ks[10], (DEPTH, DA_HEAD_DIM), 0.1),
        'da_lq2': nrm(ks[11], (DEPTH, DA_HEAD_DIM), 0.1),
        'da_lk2': nrm(ks[12], (DEPTH, DA_HEAD_DIM), 0.1),
        'da_subln_g': gain(ks[13], (DEPTH, DA_V_DIM)),
        'dn_conv_w': nrm(ks[14], (DEPTH, DN_CONV, 3 * DN_WIDTH), DN_CONV ** -0.5),
        'dn_a_log': jnp.log(jax.random.uniform(ks[15], (DEPTH, DN_HEADS), f32, 1.0, 16.0)),
        'dn_dt_bias': nrm(ks[16], (DEPTH, DN_HEADS), 0.1),
        'dn_norm_g': gain(ks[17], (DEPTH, DN_HEAD_DIM)),
        'w_br_a': nrm(ks[18], (DEPTH, GM_WIDTH, D_MODEL), GM_WIDTH ** -0.5),
        'w_br_b': nrm(ks[19], (DEPTH, DA_WIDTH, D_MODEL), DA_WIDTH ** -0.5),
        'w_br_c': nrm(ks[20], (DEPTH, DN_WIDTH, D_MODEL), DN_WIDTH ** -0.5),
        'w_out': nrm(ks[21], (DEPTH, D_MODEL, D_MODEL), D_MODEL ** -0.5),
        'ple_norm_g': gain(ks[22], (DEPTH, D_MODEL)),
        'w_ple_gate': nrm(ks[23], (DEPTH, D_MODEL, D_MODEL), D_MODEL ** -0.5),
        'w_ple_proj': nrm(ks[24], (DEPTH, PLE_DIM, D_MODEL), PLE_DIM ** -0.5),
        'final_norm_g': gain(ks[25], (D_MODEL,)),
    }


def reference(x, p, positions, norm_g, w_in, gm_ln_g, gm_ln_b, gm_ws, gm_bs, da_lq1, da_lk1, da_lq2, da_lk2, da_subln_g, dn_conv_w, dn_a_log, dn_dt_bias, dn_norm_g, w_br_a, w_br_b, w_br_c, w_out, ple_norm_g, w_ple_gate, w_ple_proj, final_norm_g):
    split_idx = np.cumsum(SPLIT_SIZES)[:-1].tolist()
    for i in range(DEPTH):
        lambda_init = 0.8 - 0.6 * math.exp(-0.3 * i)
        h = rms_norm(x, norm_g[i])
        proj = h @ w_in[i]
        (gm_u, gm_v, gm_z, da_q, da_k, da_v, da_z, dn_qkv, dn_a, dn_b, dn_z, gates) = jnp.split(proj, split_idx, axis=-1)
        y_a = chunked_gmlp(gm_u, gm_v, gm_ln_g[i], gm_ln_b[i], gm_ws[i], gm_bs[i]) * jax.nn.silu(gm_z)
        y_b = diff_attention(da_q, da_k, da_v, positions, da_lq1[i], da_lk1[i], da_lq2[i], da_lk2[i], da_subln_g[i], lambda_init) * jax.nn.silu(da_z)
        y_c = gated_deltanet(dn_qkv, dn_a, dn_b, dn_conv_w[i], dn_a_log[i], dn_dt_bias[i], dn_norm_g[i]).astype(x.dtype) * jax.nn.silu(dn_z)
        g_a, g_b, g_c = jnp.split(jax.nn.sigmoid(gates), N_BRANCH, axis=-1)
        merged = g_a * (y_a @ w_br_a[i]) + g_b * (y_b @ w_br_b[i]) + g_c * (y_c @ w_br_c[i])
        x = x + merged @ w_out[i]
        ple_gate = jax.nn.sigmoid(rms_norm(x, ple_norm_g[i]) @ w_ple_gate[i])
        x = x + ple_gate * (p[i] @ w_ple_proj[i])
    return rms_norm(x, final_norm_g)
```

```python
import math
from contextlib import ExitStack
import numpy as np
import concourse.bass as bass
import concourse.mybir as mybir
from concourse.bass_utils import run_bass_kernel_spmd

F32 = mybir.dt.float32
BF16 = mybir.dt.bfloat16
I32 = mybir.dt.int32
AF = mybir.ActivationFunctionType
ALU = mybir.AluOpType
AX = mybir.AxisListType

D = 1024
S = 2048
L = 4
NSEQ = 2
NT = 16
NTB = 4
KC = 8
EPS = 1e-6
N_IN = 8712
GM_U, GM_V, GM_Z = 0, 512, 1024
DA_Q, DA_K, DA_V, DA_Z = 1536, 2048, 2560, 3072
DN_Q, DN_K, DN_V, DN_A, DN_B, DN_Z = 3584, 4096, 4608, 5120, 5124, 5128
GATES = 5640
ROPE_THETA = 500000.0
NEG = -30000.0

C_NG, C_PG, C_FG, C_LNG, C_LNB, C_CONV = 0, 32, 64, 72, 88, 104
NCOL = 104 + 192 + 4
C_DNG = 104 + 192
R_SUBLN, R_DNG, R_ALOG, R_DTB = 0, 512, 1024, 1040
NROW = 1056


class Buf:
    __slots__ = ("w", "r", "name", "excl")

    def __init__(self, name="", excl=False):
        self.w = None
        self.r = {}
        self.name = name
        self.excl = excl


class Eng:
    MAXC = 30000

    def __init__(self, nc, e, name, is_pe=False):
        self.nc, self.e, self.name, self.is_pe = nc, e, name, is_pe
        self.nsem = 0
        self.new_sem()
        self.waited = {}
        self.pending = False

    def new_sem(self):
        self.sem = self.nc.alloc_semaphore(name=f"s_{self.name}_{self.nsem}")
        self.nsem += 1
        self.count = 0

    def wait(self, tok):
        sem, val, owner = tok
        if owner is self and self.is_pe:
            return
        k = id(sem)
        if self.waited.get(k, 0) >= val:
            return
        self.e.wait_ge(sem, val)
        self.waited[k] = val

    def needed(self, toks):
        best = {}
        for sem, val, owner in toks:
            if owner is self and self.is_pe:
                continue
            k = id(sem)
            if self.waited.get(k, 0) >= val:
                continue
            if k not in best or best[k][1] < val:
                best[k] = (sem, val)
        out = list(best.values())
        for sem, val in out:
            self.waited[id(sem)] = val
        return out

    def cur_tok(self):
        return (self.sem, self.count, self)


class Slot:
    def __init__(self, nc, name):
        self.sem = nc.alloc_semaphore(name=name)
        self.count = 0


class Queue:
    def __init__(self, nc, eng, name, nslots):
        self.eng = eng
        self.slots = [Slot(nc, f"q_{name}_{i}") for i in range(nslots)]
        self.rr = 0


def _deps(R, W):
    deps = []
    for b in R:
        if b.w is not None:
            deps.append(b.w)
    for b in W:
        if b.w is not None:
            deps.append(b.w)
        deps.extend(b.r.values())
    return deps


def _update(tok, R, W):
    owner = tok[2]
    for b in R:
        b.r[id(owner)] = tok
    for b in W:
        b.w = tok
        b.r = {}


def issue(E, fn, R=(), W=(), inc=True):
    if any(b.excl for b in R):
        W = list(W) + [b for b in R if b.excl]
        R = [b for b in R if not b.excl]
    if E.count >= Eng.MAXC and not E.pending:
        E.new_sem()
    nd = E.needed(_deps(R, W))
    for sem, val in nd[:-1]:
        E.e.wait_ge(sem, val)
    ins = fn()
    if nd:
        ins.wait_op(nd[-1][0], nd[-1][1], "sem-ge")
    if inc:
        E.count += 1
        ins.then_inc(E.sem, 1)
        E.pending = False
        tok = (E.sem, E.count, E)
    else:
        assert E.is_pe
        E.pending = True
        tok = (E.sem, E.count + 1, E)
    _update(tok, R, W)
    return tok


def dma(Q, out, in_, R=(), W=()):
    E = Q.eng
    slot = Q.slots[Q.rr]
    Q.rr = (Q.rr + 1) % len(Q.slots)
    if slot.count > 0:
        E.wait((slot.sem, slot.count, slot))
    for tok in _deps(R, W):
        E.wait(tok)
    E.e.dma_start(out=out, in_=in_).then_inc(slot.sem, 16)
    slot.count += 16
    tok = (slot.sem, slot.count, slot)
    _update(tok, R, W)
    return tok


class TT:
    def __init__(self, t, name):
        self.t = t
        self.b = Buf(name)
        self.rb = {}
        self.name = name

    def buf(self, key):
        if key not in self.rb:
            self.rb[key] = Buf(f"{self.name}{key}")
        return self.rb[key]


def build(nl=L, nseq=NSEQ, debug=False, upto=6):
    nc = bass.Bass("TRN2", target_bir_lowering=False)
    dt = nc.dram_tensor
    xT_d = dt("xT", [NSEQ, D, S], F32, kind="ExternalInput").ap()
    pT_d = dt("pT", [L, NSEQ, 256, S], F32, kind="ExternalInput").ap()
    pos_d = dt("pos", [NSEQ, S], I32, kind="ExternalInput").ap()
    w_in_d = dt("w_in", [L, D, N_IN], F32, kind="ExternalInput").ap()
    w_br_d = dt("w_br", [L, 3, 512, D], F32, kind="ExternalInput").ap()
    w_out_d = dt("w_out", [L, D, D], F32, kind="ExternalInput").ap()
    w_pg_d = dt("w_pg", [L, D, D], F32, kind="ExternalInput").ap()
    w_pp_d = dt("w_pp", [L, 256, D], F32, kind="ExternalInput").ap()
    wsT_d = dt("gm_wsT", [L, 4, 128, 128], F32, kind="ExternalInput").ap()
    bs_d = dt("gm_bs", [L, 512], F32, kind="ExternalInput").ap()
    lnb_d = dt("gm_lnb", [L, 512], F32, kind="ExternalInput").ap()
    colp_d = dt("colp", [128, NCOL], F32, kind="ExternalInput").ap()
    rowp_d = dt("rowp", [1, NROW], F32, kind="ExternalInput").ap()
    lamp_d = dt("lamp", [4, L * 64], F32, kind="ExternalInput").ap()
    xs_d = dt("xs", [NSEQ, D, S], F32, kind="Internal").ap()
    yT_d = dt("yT", [NSEQ, D, S], F32, kind="ExternalOutput").ap()
    dbg_d = None
    if debug:
        dbg_d = dt("dbg", [128, 12, S], BF16, kind="ExternalOutput").ap()
        dbgx_d = dt("dbgx", [2, D, S], F32, kind="ExternalOutput").ap()

    es = ExitStack()
    PE = Eng(nc, nc.tensor, "pe", is_pe=True)
    ACT = Eng(nc, nc.scalar, "act")
    DVE = Eng(nc, nc.vector, "dve")
    POOL = Eng(nc, nc.gpsimd, "pool")
    SP = Eng(nc, nc.sync, "sp")
    QS = Queue(nc, SP, "sp", 12)
    QW = Queue(nc, POOL, "pw", 8)

    uniq = {"n": 0}

    def sb(name, shape, dtype, stack=es):
        uniq["n"] += 1
        return TT(stack.enter_context(nc.sbuf_tensor(f"sb{uniq['n']}_{name}", shape, dtype)), name)

    banks = [es.enter_context(nc.psum_tensor(f"bank{i}", [128, 512], F32)) for i in range(8)]
    qbuf = []
    for i in range(8):
        _b = Buf(f"ps{i}", excl=True)
        qbuf.append([_b, _b, _b, _b])

    def bank_bufs(i):
        return [qbuf[i][0]]

    class Ring:
        def __init__(self, items):
            self.items = items
            self.i = 0

        def next(self):
            it = self.items[self.i]
            self.i = (self.i + 1) % len(self.items)
            return it

    def act(out, in_, func, R, W, bias=None, scale=None, accum_out=None):
        kw = {}
        if bias is not None:
            kw["bias"] = bias
        if scale is not None:
            kw["scale"] = scale
        if accum_out is not None:
            kw["accum_out"] = accum_out
        return issue(ACT, lambda: nc.scalar.activation(out=out, in_=in_, func=func, **kw), R, W)

    import os as _os
    NOPOOL = _os.environ.get("NOPOOL", "0") == "1"

    def vtt(out, in0, in1, op, R, W, E=None):
        E = E or DVE
        if NOPOOL:
            E = DVE
        return issue(E, lambda: E.e.tensor_tensor(out=out, in0=in0, in1=in1, op=op), R, W)

    def vts(out, in0, s1, op0, R, W, s2=None, op1=None, E=None):
        E = E or DVE
        if NOPOOL:
            E = DVE
        if op1 is None:
            return issue(E, lambda: E.e.tensor_scalar(out=out, in0=in0, scalar1=s1, scalar2=None, op0=op0), R, W)
        return issue(E, lambda: E.e.tensor_scalar(out=out, in0=in0, scalar1=s1, scalar2=s2, op0=op0, op1=op1), R, W)

    def vstt(out, in0, scalar, in1, op0, op1, R, W):
        return issue(DVE, lambda: nc.vector.scalar_tensor_tensor(out=out, in0=in0, scalar=scalar, in1=in1, op0=op0, op1=op1), R, W)

    def vcopy(out, in_, R, W, E=None):
        E = E or DVE
        return issue(E, lambda: E.e.tensor_copy(out=out, in_=in_), R, W)

    def mm(out, lhsT, rhs, start, stop, R, W, inc):
        return issue(PE, lambda: nc.tensor.matmul(out, lhsT=lhsT, rhs=rhs, start=start, stop=stop, skip_group_check=True), R, W, inc=inc)

    def barrier():
        engs = [PE, ACT, DVE, POOL, SP]
        assert not PE.pending
        toks = [e.cur_tok() for e in engs if e.count > 0]
        for q in (QS,):
            for sl in q.slots:
                if sl.count > 0:
                    toks.append((sl.sem, sl.count, sl))
        for e in (PE, ACT, DVE, SP, POOL):
            for tk in toks:
                if tk[2] is not e:
                    e.wait(tk)

    ident_bf = sb("ident_bf", [128, 128], BF16)
    ones_bf = sb("ones_bf", [128, 128], BF16)
    U_bf = sb("U_bf", [128, 128], BF16)
    Y_bf = sb("Y_bf", [128, 128], BF16)
    MnS_bf = sb("MnS_bf", [128, 128], BF16)
    MnST_bf = sb("MnST_bf", [128, 128], BF16)
    MnIT_bf = sb("MnIT_bf", [128, 128], BF16)
    Psw_bf = sb("Psw_bf", [128, 128], BF16)
    U_f = sb("U_f", [128, 128], F32)
    ones_f = sb("ones_f", [128, 128], F32)
    zeros_f = sb("zeros_f", [128, 128], F32)
    e0_f = sb("e0_f", [128, 1], F32)
    cc = sb("cc", [128, 8], F32)
    colp = sb("colp", [128, NCOL], F32)
    rowb = sb("rowb", [128, NROW], F32)
    lam_neg = sb("lam_neg", [128, L], F32)
    negealog = sb("negealog", [128, L * 4], F32)
    rotc = sb("rotc", [128, 4], F32)
    CONSTS = [ident_bf.b, ones_bf.b, U_bf.b, Y_bf.b, MnS_bf.b, MnST_bf.b, MnIT_bf.b, Psw_bf.b, U_f.b, ones_f.b,
              zeros_f.b, e0_f.b, cc.b, colp.b, rowb.b, lam_neg.b, negealog.b, rotc.b]

    g = nc.gpsimd

    def pool(fn, R, W):
        return issue(POOL, fn, R, W)

    pool(lambda: g.memset(ones_f.t[:], 1.0), [], [ones_f.b])
    pool(lambda: g.memset(zeros_f.t[:], 0.0), [], [zeros_f.b])
    pool(lambda: g.memset(ones_bf.t[:], 1.0), [], [ones_bf.b])
    pool(lambda: g.memset(e0_f.t[:], 0.0), [], [e0_f.b])
    pool(lambda: g.memset(e0_f.t[0:1, :], 1.0), [], [e0_f.b])
    pool(lambda: g.memset(cc.t[:, 0:1], EPS), [], [cc.b])
    pool(lambda: g.memset(cc.t[:, 1:2], 1.0), [], [cc.b])
    pool(lambda: g.memset(cc.t[:, 2:3], math.log(128.0 ** -0.5)), [], [cc.b])
    pool(lambda: g.memset(cc.t[:, 3:4], 0.0), [], [cc.b])

    def affsel(out_t, in_t, pat, cmp, fill, base, cm):
        pool(lambda: g.affine_select(out=out_t.t[:], in_=in_t.t[:], pattern=pat, compare_op=cmp, fill=fill, base=base,
                                     channel_multiplier=cm), [in_t.b], [out_t.b])

    affsel(ident_bf, ones_f, [[1, 128]], ALU.is_equal, 0.0, 0, -1)
    affsel(U_bf, ones_f, [[1, 128]], ALU.is_ge, 0.0, 0, -1)
    affsel(U_f, ones_f, [[1, 128]], ALU.is_ge, 0.0, 0, -1)
    affsel(Y_bf, ones_f, [[-1, 128]], ALU.is_gt, 0.0, 0, 1)
    affsel(MnS_bf, zeros_f, [[-1, 128]], ALU.is_gt, NEG, 0, 1)
    affsel(MnST_bf, zeros_f, [[1, 128]], ALU.is_gt, NEG, 0, -1)
    affsel(MnIT_bf, zeros_f, [[1, 128]], ALU.is_ge, NEG, 0, -1)
    with ExitStack() as st:
        bA = sb("bA", [128, 128], BF16, st)
        bB = sb("bB", [128, 128], BF16, st)
        affsel(bA, ones_f, [[1, 128]], ALU.is_equal, 0.0, 8, -1)
        affsel(bB, ones_f, [[1, 128]], ALU.is_equal, 0.0, -8, -1)
        pool(lambda: g.memset(Psw_bf.t[:], 0.0), [], [Psw_bf.b])
        for c0 in (0, 64):
            pool(lambda c0=c0: g.tensor_copy(out=Psw_bf.t[:, c0:c0 + 8], in_=bA.t[:, c0:c0 + 8]), [bA.b], [Psw_bf.b])
            pool(lambda c0=c0: g.tensor_copy(out=Psw_bf.t[:, c0 + 8:c0 + 16], in_=bB.t[:, c0 + 8:c0 + 16]), [bB.b], [Psw_bf.b])
        pidx = sb("pidx", [128, 1], I32, st)
        pt_i = sb("pt_i", [128, 2], I32, st)
        pt_f = sb("pt_f", [128, 4], F32, st)
        pool(lambda: g.iota(pidx.t[:], pattern=[[0, 1]], base=0, channel_multiplier=1), [], [pidx.b])
        issue(DVE, lambda: nc.vector.tensor_single_scalar(out=pt_i.t[:, 0:1], in_=pidx.t[:], scalar=63, op=ALU.bitwise_and), [pidx.b], [pt_i.b])
        issue(DVE, lambda: nc.vector.tensor_single_scalar(out=pt_i.t[:, 1:2], in_=pidx.t[:], scalar=7, op=ALU.bitwise_and), [pidx.b], [pt_i.b])
        vcopy(pt_f.t[:, 0:2], pt_i.t[:, 0:2], [pt_i.b], [pt_f.b])
        act(pt_f.t[:, 2:3], pt_f.t[:, 1:2], AF.Exp, [pt_f.b], [pt_f.b], scale=-math.log(ROPE_THETA) / 8.0)
        vts(rotc.t[:, 0:1], pt_f.t[:, 2:3], 1.0 / (2.0 * math.pi), ALU.mult, [pt_f.b], [rotc.b])
        vts(rotc.t[:, 2:3], pt_f.t[:, 0:1], 16.0, ALU.is_lt, [pt_f.b], [rotc.b])
        vts(pt_f.t[:, 3:4], pt_f.t[:, 0:1], 8.0, ALU.is_ge, [pt_f.b], [pt_f.b], s2=2.0, op1=ALU.mult)
        vstt(rotc.t[:, 1:2], pt_f.t[:, 3:4], -1.0, rotc.t[:, 2:3], ALU.add, ALU.mult, [pt_f.b, rotc.b], [rotc.b])
        vts(rotc.t[:, 3:4], rotc.t[:, 2:3], -1.0, ALU.mult, [rotc.b], [rotc.b], s2=1.0, op1=ALU.add)
        dma(QS, colp.t[:], colp_d, [], [colp.b])
        dma(QS, rowb.t[:], rowp_d.partition_broadcast(128), [], [rowb.b])
        lamb = sb("lamb", [128, 4, L * 64], F32, st)
        for i in range(4):
            dma(QS, lamb.t[:, i, :], lamp_d[i:i + 1, :].partition_broadcast(128), [], [lamb.b])
        lprod = sb("lprod", [128, 2, L * 64], F32, st)
        lsum = sb("lsum", [128, 2 * L], F32, st)
        vtt(lprod.t[:, 0, :], lamb.t[:, 0, :], lamb.t[:, 1, :], ALU.mult, [lamb.b], [lprod.b])
        vtt(lprod.t[:, 1, :], lamb.t[:, 2, :], lamb.t[:, 3, :], ALU.mult, [lamb.b], [lprod.b])
        for i in range(2):
            for l in range(L):
                issue(DVE, lambda i=i, l=l: nc.vector.tensor_reduce(out=lsum.t[:, i * L + l:i * L + l + 1], in_=lprod.t[:, i, l * 64:(l + 1) * 64],
                                                                     axis=AX.X, op=ALU.add), [lprod.b], [lsum.b])
        act(lsum.t[:], lsum.t[:], AF.Exp, [lsum.b], [lsum.b])
        for l in range(L):
            li = 0.8 - 0.6 * math.exp(-0.3 * l)
            vstt(lam_neg.t[:, l:l + 1], lsum.t[:, L + l:L + l + 1], -li, lsum.t[:, l:l + 1], ALU.add, ALU.subtract, [lsum.b], [lam_neg.b])
        act(negealog.t[:], rowb.t[:, R_ALOG:R_ALOG + 16], AF.Exp, [rowb.b], [negealog.b])
        vts(negealog.t[:], negealog.t[:], -1.0, ALU.mult, [negealog.b], [negealog.b])
        barrier()

    xb = sb("xb", [128, KC, S], BF16)
    rstd_bc = sb("rstd_bc", [128, S], F32)
    rstd_col = sb("rstd_col", [128, NT], F32)
    Y = {}
    LY = {"st": None}
    COS = sb("COS", [128, S], BF16)
    SIN = sb("SIN", [128, S], BF16)
    NSLOT = 4
    slots = [sb(f"wslot{i}", [128, KC, 512], BF16) for i in range(NSLOT)]
    xin = [sb(f"xin{i}", [128, 512], F32) for i in range(3)]
    xin_r = Ring(xin)
    tmpf = [sb(f"tmpf{i}", [128, 512], F32) for i in range(4)]
    tmpf_r = Ring(tmpf)
    tmpb = [sb(f"tmpb{i}", [128, 512], BF16) for i in range(3)]
    tmpb_r = Ring(tmpb)
    small = [sb(f"small{i}", [128, 8], F32) for i in range(6)]
    small_r = Ring(small)
    Et = sb("Et", [128, 4, 128], F32)
    WTg = sb("WTg", [128, 4, 128], BF16)

    def wsrc(ap2d):
        return ap2d.rearrange("(k p) n -> p k n", p=128)

    def plan():
        P = []
        for s in range(nseq):
            for l in range(nl):
                W = w_in_d[l]
                P.append((("ab", s, l), [((0, KC, 0, 8), wsrc(W[:, DN_A:DN_A + 8]))]))
                for h in range(4):
                    P.append((("dn", s, l, h), [((0, KC, i * 128, 128), wsrc(W[:, c + h * 128:c + (h + 1) * 128]))
                                                for i, c in enumerate((DN_Q, DN_K, DN_V, DN_Z))]))
                for h in range(4):
                    P.append((("at", s, l, h), [((0, KC, i * 128, 128), wsrc(W[:, c + h * 128:c + (h + 1) * 128]))
                                                for i, c in enumerate((DA_Q, DA_K, DA_V, DA_Z))]))
                for nm, c in (("gu", GM_U), ("gz", GM_Z), ("gv", GM_V)):
                    P.append(((nm, s, l), [((0, KC, 0, 512), wsrc(W[:, c:c + 512]))]))
                for j in range(8):
                    P.append((("mg", s, l, j), [((0, KC, i * 128, 128), wsrc(W[:, GATES + i * 1024 + j * 128:GATES + i * 1024 + (j + 1) * 128]))
                                                for i in range(3)]))
                    P.append((("mb", s, l, j), [((0, 4, i * 128, 128), wsrc(w_br_d[l, i][:, j * 128:(j + 1) * 128])) for i in range(3)]))
                for j in range(8):
                    P.append((("wo", s, l, j), [((0, KC, 0, 128), wsrc(w_out_d[l][:, j * 128:(j + 1) * 128]))]))
                for j in range(8):
                    P.append((("pg", s, l, j), [((0, KC, 0, 128), wsrc(w_pg_d[l][:, j * 128:(j + 1) * 128])),
                                                ((0, 2, 128, 128), wsrc(w_pp_d[l][:, j * 128:(j + 1) * 128]))]))
        lvl = {'ab': 3, 'dn': 3, 'at': 4, 'gu': 5, 'gz': 5, 'gv': 5, 'mg': 6, 'mb': 6, 'wo': 6, 'pg': 6}
        return [e for e in P if lvl[e[0][0]] <= upto]

    PLAN = plan()
    wstate = {"emit": 0, "use": 0}

    def wget(key):
        k = wstate["use"]
        assert PLAN[k][0] == key, (PLAN[k][0], key)
        while wstate["emit"] < min(len(PLAN), k + NSLOT - 1):
            i = wstate["emit"]
            sl = slots[i % NSLOT]
            for (k0, k1, c0, n), src in PLAN[i][1]:
                dma(QW, sl.t[:, k0:k1, c0:c0 + n], src, [], [sl.b])
            wstate["emit"] += 1
        wstate["use"] += 1
        return slots[k % NSLOT]

    def run_pipelined(gen_iter, width):
        active = []
        it = iter(gen_iter)
        done = False
        while True:
            while len(active) < width and not done:
                try:
                    active.append(next(it))
                except StopIteration:
                    done = True
            if not active:
                break
            for g_ in list(active):
                try:
                    next(g_)
                except StopIteration:
                    active.remove(g_)

    def rstd_from_ps(out_ap, ps_ap, n_div, R, W, bias_extra=None, inplace=False):
        shp = ps_ap.shape[-1]
        if inplace:
            act(out_ap, ps_ap, AF.Ln, R, W, bias=cc.t[:, 0:1], scale=1.0 / n_div)
            if bias_extra is None:
                act(out_ap, out_ap, AF.Exp, W, W, scale=-0.5)
            else:
                act(out_ap, out_ap, AF.Exp, W, W, scale=-0.5, bias=bias_extra)
            return
        t = tmpf_r.next()
        act(t.t[:, 0:shp], ps_ap, AF.Ln, R, [t.b], bias=cc.t[:, 0:1], scale=1.0 / n_div)
        if bias_extra is None:
            act(out_ap, t.t[:, 0:shp], AF.Exp, [t.b], W, scale=-0.5)
        else:
            act(out_ap, t.t[:, 0:shp], AF.Exp, [t.b], W, scale=-0.5, bias=bias_extra)

    def small_rstd(ss_ap, n_div, R, sm):
        act(ss_ap, ss_ap, AF.Ln, R, [sm.b], bias=cc.t[:, 0:1], scale=1.0 / n_div)
        act(ss_ap, ss_ap, AF.Exp, [sm.b], [sm.b], scale=-0.5)

    def p0(s, l, pj):
        import os
        LV = int(os.environ.get("P0LVL", "9"))
        src = xT_d[s] if l == 0 else xs_d[s]
        p0st = ExitStack()
        p0ring = Ring(xin + [sb(f"p0x{i}", [128, 512], F32, p0st) for i in range(6)])
        for tb in range(NTB):
            bi = pj.next()
            for j in range(KC):
                xt = p0ring.next()
                dma(QS, xt.t[:], src[j * 128:(j + 1) * 128, tb * 512:(tb + 1) * 512], [], [xt.b])
                if LV >= 2:
                    vts(xb.t[:, j, tb * 512:(tb + 1) * 512], xt.t[:], colp.t[:, C_NG + l * 8 + j:C_NG + l * 8 + j + 1], ALU.mult, [xt.b, colp.b], [xb.buf(tb)])
                if LV >= 3:
                    sq = tmpb_r.next()
                    act(sq.t[:], xt.t[:], AF.Square, [xt.b], [sq.b])
                if LV >= 4:
                    mm(banks[bi][:], ones_bf.t[:], sq.t[:], j == 0, j == KC - 1, [ones_bf.b, sq.b], bank_bufs(bi), inc=True)
            if LV >= 5:
                rstd_from_ps(rstd_bc.t[:, tb * 512:(tb + 1) * 512], banks[bi][:], float(D), bank_bufs(bi), [rstd_bc.buf(tb)])
        if LV >= 6:
            bi = pj.next()
            for t in range(NT):
                mm(banks[bi][:, t:t + 1], rstd_bc.t[:, t * 128:(t + 1) * 128], e0_f.t[:], True, True, [rstd_bc.buf(t // 4), e0_f.b], [qbuf[bi][0]], inc=True)
            vcopy(rstd_col.t[:], banks[bi][:, 0:NT], [qbuf[bi][0]], [rstd_col.b])
        barrier()
        p0st.close()

    def proj_fm(W, c0, tb, bi):
        for kc in range(KC):
            mm(banks[bi][:], W.t[:, kc, c0:c0 + 128], xb.t[:, kc, tb * 512:(tb + 1) * 512], kc == 0, kc == KC - 1,
               [W.b, xb.buf(tb)], bank_bufs(bi), inc=(kc == KC - 1))

    def proj_tm(W, c0, n, t, out_ap, W_bufs):
        for kc in range(KC):
            mm(out_ap, xb.t[:, kc, t * 128:(t + 1) * 128], W.t[:, kc, c0:c0 + n], kc == 0, kc == KC - 1,
               [W.b, xb.buf(t // 4)], W_bufs, inc=(kc == KC - 1))

    def rotary_tables(s):
        with ExitStack() as st:
            posi = sb("posi", [128, S], I32, st)
            tt_ = sb("rt_t", [128, S], F32, st)
            t2 = sb("rt_t2", [128, S], F32, st)
            dma(QS, posi.t[:], pos_d[s:s + 1, :].partition_broadcast(128), [], [posi.b])
            vcopy(tt_.t[:], posi.t[:], [posi.b], [tt_.b])
            vts(tt_.t[:], tt_.t[:], rotc.t[:, 0:1], ALU.mult, [tt_.b, rotc.b], [tt_.b])
            for which in (0, 1):
                if which == 1:
                    vts(tt_.t[:], tt_.t[:], 0.25, ALU.add, [tt_.b], [tt_.b])
                vcopy(posi.t[:], tt_.t[:], [tt_.b], [posi.b])
                vcopy(t2.t[:], posi.t[:], [posi.b], [t2.b])
                vtt(t2.t[:], tt_.t[:], t2.t[:], ALU.subtract, [tt_.b, t2.b], [t2.b])
                act(t2.t[:], t2.t[:], AF.Sin, [t2.b], [t2.b], scale=2.0 * math.pi * (1.0 - 2e-6))
                if which == 0:
                    vts(SIN.t[:], t2.t[:], rotc.t[:, 1:2], ALU.mult, [t2.b, rotc.b], [SIN.b])
                else:
                    vts(COS.t[:], t2.t[:], rotc.t[:, 2:3], ALU.mult, [t2.b, rotc.b], [COS.b], s2=rotc.t[:, 3:4], op1=ALU.add)
            barrier()

    def deltanet_phase(s, l):
        pj = Ring([0, 1])
        Y['c'] = sb("yTc", [128, 4, S], BF16, LY["st"])
        with ExitStack() as st:
            xc = [sb(f"dn_xc{i}", [128, 4 + 512], BF16, st) for i in range(4)]
            dacc_r = Ring([sb(f"dn_acc{i}", [128, 512], F32, st) for i in range(6)])
            dg_r = Ring([sb(f"dn_dg{i}", [128, 128], BF16, st) for i in range(8)])
            pjp = Ring([0, 1, 2, 3, 4, 5, 6, 7])
            ab = sb("dn_ab", [128, NT, 8], F32, st)
            gall = sb("dn_g", [128, NT, 4], F32, st)
            lnb = sb("dn_lnb", [128, NT, 4], F32, st)
            gcgl = sb("dn_gcgl", [128, NT, 8], F32, st)
            ex = sb("dn_ex", [128, 5, NT, 4], F32, st)
            NSTG = 2

            def stage_tiles(i, sl):
                d = {}
                for nm in ("X", "Dl", "Db", "DTb", "decT", "egcb", "N", "NT", "qkT", "qdT", "Rk", "kd", "Rv", "TTa", "TTb",
                           "Pa", "Pb", "PTa", "PTb", "nwT"):
                    d[nm] = sb(f"dn_{nm}{i}_{sl}", [128, 128], BF16, st)
                return d

            qkvT_l = [sb(f"dn_qkvT{i}", [128, 3, S], BF16, st) for i in range(2)]
            zs_l = [sb(f"dn_zs{i}", [128, S], BF16, st) for i in range(2)]
            Sf_l = [sb(f"dn_S{i}", [128, 128], F32, st) for i in range(2)]
            Sb_l = [sb(f"dn_Sb{i}", [128, 128], BF16, st) for i in range(2)]
            stg_l = [[stage_tiles(i, sl) for i in range(NSTG)] for sl in range(2)]
            vn_l = [[sb(f"dn_vn{i}_{sl}", [128, 128], BF16, st) for i in range(2)] for sl in range(2)]
            yc_l = [[sb(f"dn_yc{i}_{sl}", [128, 128], BF16, st) for i in range(2)] for sl in range(2)]

            qring = Ring([(b, 0) for b in range(2, 8)])

            def qt():
                b, q = qring.next()
                return banks[b][:, q * 128:(q + 1) * 128], qbuf[b][q]

            W = wget(("ab", s, l))
            bi = pj.next()
            for t in range(NT):
                proj_tm(W, 0, 8, t, banks[bi][:, t * 8:(t + 1) * 8], [qbuf[bi][0]])
            for t in range(NT):
                vts(ab.t[:, t, :], banks[bi][:, t * 8:(t + 1) * 8], rstd_col.t[:, t:t + 1], ALU.mult, [qbuf[bi][0], rstd_col.b], [ab.b])
            sp = tmpf_r.next()
            spv = sp.t[:, 0:NT * 8].rearrange("p (t c) -> p t c", c=8)
            for h in range(4):
                vts(spv[:, :, h], ab.t[:, :, h], rowb.t[:, R_DTB + l * 4 + h:R_DTB + l * 4 + h + 1], ALU.add, [ab.b, rowb.b], [sp.b])
            vcopy(spv[:, :, 4:8], ab.t[:, :, 4:8], [ab.b], [sp.b])
            act(spv[:, :, 0:4], spv[:, :, 0:4], AF.Exp, [sp.b], [sp.b])
            act(spv[:, :, 4:8], spv[:, :, 4:8], AF.Exp, [sp.b], [sp.b], scale=-1.0)
            act(sp.t[:, 0:NT * 8], sp.t[:, 0:NT * 8], AF.Ln, [sp.b], [sp.b], bias=cc.t[:, 1:2])
            for h in range(4):
                vts(gall.t[:, :, h], spv[:, :, h], negealog.t[:, l * 4 + h:l * 4 + h + 1], ALU.mult, [sp.b, negealog.b], [gall.b])
            vts(lnb.t[:], spv[:, :, 4:8], -1.0, ALU.mult, [sp.b], [lnb.b])
            bi = pj.next()
            for t in range(NT):
                mm(banks[bi][:, t * 8:t * 8 + 4], U_f.t[:], gall.t[:, t, :], True, True, [U_f.b, gall.b], [qbuf[bi][0]], inc=True)
                mm(banks[bi][:, t * 8 + 4:t * 8 + 8], ones_f.t[:], gall.t[:, t, :], True, True, [ones_f.b, gall.b], [qbuf[bi][0]], inc=True)
            vcopy(gcgl.t[:], banks[bi][:, 0:NT * 8].rearrange("p (t c) -> p t c", c=8), [qbuf[bi][0]], [gcgl.b])
            gc = gcgl.t[:, :, 0:4]
            gl = gcgl.t[:, :, 4:8]
            vcopy(ex.t[:, 0], gc, [gcgl.b], [ex.b])
            vtt(ex.t[:, 1], gc, lnb.t[:], ALU.add, [gcgl.b, lnb.b], [ex.b])
            vtt(ex.t[:, 2], gl, gc, ALU.subtract, [gcgl.b], [ex.b])
            vcopy(ex.t[:, 3], lnb.t[:], [lnb.b], [ex.b])
            vcopy(ex.t[:, 4], gl, [gcgl.b], [ex.b])
            act(ex.t[:], ex.t[:], AF.Exp, [ex.b], [ex.b])

            import os
            DLV = int(os.environ.get("DNLVL", "9"))
            Wd = {}
            nblk = {"n": 0}

            def getW(h):
                if h not in Wd:
                    Wd[h] = wget(("dn", s, l, h))
                return Wd[h]

            def qkv_block(h, qkvT, ci, tb, dgs):
                W = getW(h)
                if tb == 0:
                    chunk = (0, 4, 8)[ci] + h
                    for j in range(4):
                        dg_ = dg_r.next()
                        vts(dg_.t[:], ident_bf.t[:], colp.t[:, C_CONV + (l * 4 + j) * 12 + chunk:C_CONV + (l * 4 + j) * 12 + chunk + 1],
                            ALU.mult, [ident_bf.b, colp.b], [dg_.b], E=POOL)
                        dgs.append(dg_)
                n_ = nblk["n"]
                nblk["n"] += 1
                xcur = xc[n_ % 4]
                xprev = xc[(n_ - 1) % 4]
                bi = pjp.next()
                proj_fm(W, ci * 128, tb, bi)
                if tb == 0:
                    issue(DVE, lambda xcur=xcur: nc.vector.memset(xcur.t[:, 0:4], 0.0), [], [xcur.b])
                else:
                    vcopy(xcur.t[:, 0:4], xprev.t[:, 512:516], [xprev.b], [xcur.b], E=POOL)
                vtt(xcur.t[:, 4:516], banks[bi][:], rstd_bc.t[:, tb * 512:(tb + 1) * 512], ALU.mult,
                    bank_bufs(bi) + [rstd_bc.buf(tb)], [xcur.b])
                yield
                b3 = pjp.next()
                for j in range(4):
                    mm(banks[b3][:], dgs[j].t[:], xcur.t[:, 1 + j:1 + j + 512], j == 0, j == 3, [dgs[j].b, xcur.b], bank_bufs(b3), inc=(j == 3))
                dst = qkvT.t[:, ci, tb * 512:(tb + 1) * 512]
                if ci == 2:
                    act(dst, banks[b3][:], AF.Silu, bank_bufs(b3), [qkvT.buf((ci, tb))])
                    return
                acc = dacc_r.next()
                act(acc.t[:], banks[b3][:], AF.Silu, bank_bufs(b3), [acc.b])
                sq = tmpb_r.next()
                act(sq.t[:], acc.t[:], AF.Square, [acc.b], [sq.b])
                yield
                b2 = pjp.next()
                mm(banks[b2][:], ones_bf.t[:], sq.t[:], True, True, [ones_bf.b, sq.b], bank_bufs(b2), inc=True)
                rn = dacc_r.next()
                rstd_from_ps(rn.t[:], banks[b2][:], 1.0, bank_bufs(b2), [rn.b], bias_extra=(cc.t[:, 2:3] if ci == 0 else None), inplace=True)
                yield
                vtt(dst, acc.t[:], rn.t[:], ALU.mult, [acc.b, rn.b], [qkvT.buf((ci, tb))])

            def z_block(h, zs, tb):
                W = getW(h)
                sl_ = slice(tb * 512, (tb + 1) * 512)
                bi = pjp.next()
                proj_fm(W, 384, tb, bi)
                yield
                t1 = dacc_r.next()
                vtt(t1.t[:], banks[bi][:], rstd_bc.t[:, sl_], ALU.mult, bank_bufs(bi) + [rstd_bc.buf(tb)], [t1.b])
                yield
                act(zs.t[:, sl_], t1.t[:], AF.Silu, [t1.b], [zs.buf(tb)])

            def project_blocks(h, qkvT, zs):
                for ci in range(3):
                    dgs = []
                    for tb in range(NTB):
                        yield qkv_block(h, qkvT, ci, tb, dgs)
                for tb in range(NTB):
                    yield z_block(h, zs, tb)

            def make_scan(h, qkvT, zs, stg, vn, yc, Sf, Sb_):
                def col(k, t):
                    return ex.t[:, k, t, h:h + 1]

                def prep(t):
                    T_ = stg[t % NSTG]
                    tok = slice(t * 128, (t + 1) * 128)
                    qT_ = qkvT.t[:, 0, tok]
                    kT_ = qkvT.t[:, 1, tok]
                    vT_ = qkvT.t[:, 2, tok]
                    qb_, kb_, vb_ = qkvT.buf((0, t // 4)), qkvT.buf((1, t // 4)), qkvT.buf((2, t // 4))
                    vts(T_["X"].t[:], U_bf.t[:], gall.t[:, t, h:h + 1], ALU.mult, [U_bf.b, gall.b], [T_["X"].b])
                    yield
                    vts(T_["Dl"].t[:], ident_bf.t[:], lnb.t[:, t, h:h + 1], ALU.mult, [ident_bf.b, lnb.b], [T_["Dl"].b], E=POOL)
                    yield
                    X, Dl = T_["X"], T_["Dl"]
                    p1, b1 = qt()
                    mm(p1, X.t[:], Y_bf.t[:], True, False, [X.b, Y_bf.b], [b1], inc=False)
                    mm(p1, ident_bf.t[:], MnS_bf.t[:], False, True, [ident_bf.b, MnS_bf.b], [b1], inc=True)
                    act(T_["Db"].t[:], p1, AF.Exp, [b1, lnb.b], [T_["Db"].b], bias=lnb.t[:, t, h:h + 1])
                    yield
                    p2, b2 = qt()
                    mm(p2, Y_bf.t[:], X.t[:], True, False, [X.b, Y_bf.b], [b2], inc=False)
                    mm(p2, ones_bf.t[:], Dl.t[:], False, False, [ones_bf.b, Dl.b], [b2], inc=False)
                    mm(p2, ident_bf.t[:], MnST_bf.t[:], False, True, [ident_bf.b, MnST_bf.b], [b2], inc=True)
                    act(T_["DTb"].t[:], p2, AF.Exp, [b2], [T_["DTb"].b])
                    yield
                    p3, b3 = qt()
                    mm(p3, Y_bf.t[:], X.t[:], True, False, [X.b, Y_bf.b], [b3], inc=False)
                    mm(p3, ident_bf.t[:], MnIT_bf.t[:], False, True, [ident_bf.b, MnIT_bf.b], [b3], inc=True)
                    act(T_["decT"].t[:], p3, AF.Exp, [b3], [T_["decT"].b])
                    yield
                    p4, b4 = qt()
                    mm(p4, ones_bf.t[:], X.t[:], True, True, [ones_bf.b, X.b], [b4], inc=True)
                    act(T_["egcb"].t[:], p4, AF.Exp, [b4], [T_["egcb"].b])
                    yield
                    p5, b5 = qt()
                    mm(p5, kT_, kT_, True, True, [kb_], [b5], inc=True)
                    vstt(T_["N"].t[:], p5, -1.0, T_["Db"].t[:], ALU.mult, ALU.mult, [b5, T_["Db"].b], [T_["N"].b])
                    vstt(T_["NT"].t[:], p5, -1.0, T_["DTb"].t[:], ALU.mult, ALU.mult, [b5, T_["DTb"].b], [T_["NT"].b])
                    yield
                    p6, b6 = qt()
                    mm(p6, kT_, qT_, True, True, [kb_, qb_], [b6], inc=True)
                    vtt(T_["qkT"].t[:], p6, T_["decT"].t[:], ALU.mult, [b6, T_["decT"].b], [T_["qkT"].b])
                    yield
                    vtt(T_["qdT"].t[:], qT_, T_["egcb"].t[:], ALU.mult, [qb_, T_["egcb"].b], [T_["qdT"].b], E=POOL)
                    yield
                    p7, b7 = qt()
                    mm(p7, kT_, ident_bf.t[:], True, True, [kb_, ident_bf.b], [b7], inc=True)
                    act(T_["Rk"].t[:], p7, AF.Copy, [b7, ex.b], [T_["Rk"].b], scale=col(1, t))
                    vts(T_["kd"].t[:], p7, col(2, t), ALU.mult, [b7, ex.b], [T_["kd"].b])
                    yield
                    p8, b8 = qt()
                    mm(p8, vT_, ident_bf.t[:], True, True, [vb_, ident_bf.b], [b8], inc=True)
                    act(T_["Rv"].t[:], p8, AF.Copy, [b8, ex.b], [T_["Rv"].b], scale=col(3, t))
                    yield
                    TTc, TTn = T_["TTa"], T_["TTb"]
                    vtt(TTc.t[:], ident_bf.t[:], T_["NT"].t[:], ALU.add, [ident_bf.b, T_["NT"].b], [TTc.b])
                    yield
                    Pc, PTc = T_["N"], T_["NT"]
                    Pn, PTn = T_["Pa"], T_["PTa"]
                    Pn2, PTn2 = T_["Pb"], T_["PTb"]
                    for k in range(1, 7):
                        pp, bp = qt()
                        mm(pp, PTc.t[:], Pc.t[:], True, True, [PTc.b, Pc.b], [bp], inc=True)
                        act(Pn.t[:], pp, AF.Copy, [bp], [Pn.b])
                        yield
                        if k < 6:
                            pq, bq = qt()
                            mm(pq, Pc.t[:], PTc.t[:], True, True, [PTc.b, Pc.b], [bq], inc=True)
                            vcopy(PTn.t[:], pq, [bq], [PTn.b])
                            yield
                        pr, br = qt()
                        mm(pr, Pn.t[:], TTc.t[:], True, True, [Pn.b, TTc.b], [br], inc=True)
                        vtt(TTn.t[:], pr, TTc.t[:], ALU.add, [br, TTc.b], [TTn.b])
                        yield
                        TTc, TTn = TTn, TTc
                        Pc, PTc = Pn, PTn
                        Pn, PTn, Pn2, PTn2 = Pn2, PTn2, Pn, PTn
                    T_["TT"] = TTc
                    pw, bw = qt()
                    mm(pw, T_["Rk"].t[:], TTc.t[:], True, True, [T_["Rk"].b, TTc.b], [bw], inc=True)
                    act(T_["nwT"].t[:], pw, AF.Copy, [bw], [T_["nwT"].b], scale=-1.0)
                    yield

                def chain(t):
                    T_ = stg[t % NSTG]
                    TTc = T_["TT"]
                    v_ = vn[t % 2]
                    y_ = yc[t % 2]
                    pv, bv = qt()
                    mm(pv, TTc.t[:], T_["Rv"].t[:], True, t == 0, [TTc.b, T_["Rv"].b], [bv], inc=(t == 0))
                    if t > 0:
                        mm(pv, T_["nwT"].t[:], Sb_.t[:], False, True, [T_["nwT"].b, Sb_.b], [bv], inc=True)
                    vcopy(v_.t[:], pv, [bv], [v_.b])
                    yield
                    po, bo = qt()
                    if t > 0:
                        mm(po, T_["qdT"].t[:], Sb_.t[:], True, False, [T_["qdT"].b, Sb_.b], [bo], inc=False)
                    mm(po, T_["qkT"].t[:], v_.t[:], t == 0, True, [T_["qkT"].b, v_.b], [bo], inc=True)
                    sm = small_r.next()
                    junk = tmpf_r.next()
                    act(junk.t[:, 0:128], po, AF.Square, [bo], [junk.b, sm.b], accum_out=sm.t[:, 0:1])
                    small_rstd(sm.t[:, 0:1], 128.0, [sm.b], sm)
                    vts(y_.t[:], po, sm.t[:, 0:1], ALU.mult, [bo, sm.b], [y_.b])
                    yield
                    if t < NT - 1:
                        pS, bS = qt()
                        mm(pS, T_["kd"].t[:], v_.t[:], True, True, [T_["kd"].b, v_.b], [bS], inc=True)
                        if t == 0:
                            vcopy(Sf.t[:], pS, [bS], [Sf.b])
                        else:
                            vstt(Sf.t[:], Sf.t[:], col(4, t), pS, ALU.mult, ALU.add, [Sf.b, ex.b, bS], [Sf.b])
                        yield
                        act(Sb_.t[:], Sf.t[:], AF.Copy, [Sf.b], [Sb_.b])
                        yield
                    pt, bt = qt()
                    mm(pt, y_.t[:], ident_bf.t[:], True, True, [y_.b, ident_bf.b], [bt], inc=True)
                    vstt(Y['c'].t[:, h, t * 128:(t + 1) * 128], pt, colp.t[:, C_DNG + l:C_DNG + l + 1], zs.t[:, t * 128:(t + 1) * 128], ALU.mult, ALU.mult,
                         [bt, colp.b, zs.buf(t // 4)], [Y['c'].buf((h, t // 4))])
                    yield

                yield from prep(0)
                for t in range(NT):
                    sub = [chain(t)]
                    if t + 1 < NT:
                        sub.append(prep(t + 1))
                    while sub:
                        for g_ in list(sub):
                            try:
                                next(g_)
                                yield
                            except StopIteration:
                                sub.remove(g_)

            for hp in range(2):
                gens = []
                def pair_blocks(hp=hp):
                    for idx in range(2):
                        yield from project_blocks(2 * hp + idx, qkvT_l[idx], zs_l[idx])
                run_pipelined(pair_blocks(), 3)
                for idx in range(2):
                    h = 2 * hp + idx
                    gens.append(make_scan(h, qkvT_l[idx], zs_l[idx], stg_l[idx], vn_l[idx], yc_l[idx], Sf_l[idx], Sb_l[idx]))
                while gens:
                    for g_ in list(gens):
                        try:
                            next(g_)
                        except StopIteration:
                            gens.remove(g_)
            barrier()

    def attention_phase(s, l):
        li = 0.8 - 0.6 * math.exp(-0.3 * l)
        Y['b'] = sb("yTb", [128, 4, S], BF16, LY["st"])
        with ExitStack() as st:
            QT = sb("at_QT", [128, S], BF16, st)
            KT = sb("at_KT", [128, S], BF16, st)
            Va = sb("at_Va", [128, NT, 130], BF16, st)
            zs = sb("at_zs", [128, NT, 128], BF16, st)
            PTt = [sb(f"at_PT{i}", [128, 2, 512], BF16, st) for i in range(2)]
            osb = [sb(f"at_o{i}", [128, 128], F32, st) for i in range(2)]
            ybt = [sb(f"at_yb{i}", [128, 128], BF16, st) for i in range(3)]
            issue(DVE, lambda: nc.vector.memset(Va.t[:, :, 128:129], 1.0), [], [Va.buf(i) for i in range(NT)])
            pj = Ring([4, 5, 6, 7])
            pjq = Ring([0, 1, 2, 3, 4, 5, 6, 7])
            raw_r = Ring([sb(f"at_raw{i}", [128, 512], BF16, st) for i in range(5)])
            sset = Ring([(4, 5), (6, 7)])
            cnt = {"q": 0}
            import os
            ALV = int(os.environ.get("ATLVL", "9"))
            for h in range(4):
                W = wget(("at", s, l, h))
                def qk_block(ci, dstT, tb, W=W):
                    sl_ = slice(tb * 512, (tb + 1) * 512)
                    bi = pjq.next()
                    proj_fm(W, ci * 128, tb, bi)
                    raw = raw_r.next()
                    vtt(raw.t[:], banks[bi][:], rstd_bc.t[:, sl_], ALU.mult, bank_bufs(bi) + [rstd_bc.buf(tb)], [raw.b])
                    yield
                    b2 = pjq.next()
                    mm(banks[b2][:], Psw_bf.t[:], raw.t[:], True, True, [Psw_bf.b, raw.b], bank_bufs(b2), inc=True)
                    yield
                    t1 = tmpf_r.next()
                    vtt(t1.t[:], raw.t[:], COS.t[:, sl_], ALU.mult, [raw.b, COS.b], [t1.b], E=POOL)
                    t2 = tmpf_r.next()
                    vtt(t2.t[:], banks[b2][:], SIN.t[:, sl_], ALU.mult, bank_bufs(b2) + [SIN.b], [t2.b])
                    vtt(dstT.t[:, sl_], t1.t[:], t2.t[:], ALU.add, [t1.b, t2.b], [dstT.buf(tb)])

                def vz_block(t, W=W):
                    bi = pjq.next()
                    o_ = banks[bi][:, 0:256]
                    hb = bank_bufs(bi)
                    proj_tm(W, 256, 256, t, o_, hb)
                    yield
                    vts(Va.t[:, t, 0:128], o_[:, 0:128], rstd_col.t[:, t:t + 1], ALU.mult, hb + [rstd_col.b], [Va.buf(t)])
                    act(zs.t[:, t, :], o_[:, 128:256], AF.Silu, hb + [rstd_col.b], [zs.buf(t)], scale=rstd_col.t[:, t:t + 1])
                    yield
                    vstt(zs.t[:, t, :], zs.t[:, t, :], 1.0 - li, rowb.t[:, R_SUBLN + l * 128:R_SUBLN + (l + 1) * 128], ALU.mult, ALU.mult,
                         [zs.buf(t), rowb.b], [zs.buf(t)])

                def at_blocks():
                    for ci, dstT in ((0, QT), (1, KT)):
                        for tb in range(NTB):
                            yield qk_block(ci, dstT, tb)
                    for t in range(NT):
                        yield vz_block(t)
                run_pipelined(at_blocks(), 3)
                iters = [(qb, kt) for qb in range(NTB) for kt in range(4 * qb + 4)]
                state = {}

                def emit_S(it):
                    qb, kt = it
                    j0 = max(0, kt - 4 * qb)
                    n = 512 - j0 * 128
                    qs = qb * 512 + j0 * 128
                    sb_ = sset.next()
                    state[it] = sb_
                    for c in range(2):
                        mm(banks[sb_[c]][:, 0:n], KT.t[c * 64:(c + 1) * 64, kt * 128:(kt + 1) * 128], QT.t[c * 64:(c + 1) * 64, qs:qs + n],
                           True, True, [KT.buf(kt // 4), QT.buf(qb)], bank_bufs(sb_[c]), inc=True)

                def emit_exp(it):
                    qb, kt = it
                    j0 = max(0, kt - 4 * qb)
                    n = 512 - j0 * 128
                    sb_ = state.pop(it)
                    P_ = PTt[cnt["q"] % 2]
                    cnt["q"] += 1
                    for c in range(2):
                        act(P_.t[:, c, 0:n], banks[sb_[c]][:, 0:n], AF.Exp, bank_bufs(sb_[c]), [P_.b], scale=0.125)
                    if kt >= 4 * qb:
                        for c in range(2):
                            vtt(P_.t[:, c, 0:128], P_.t[:, c, 0:128], U_bf.t[:], ALU.mult, [P_.b, U_bf.b], [P_.b], E=POOL)
                    return P_

                def emit_AV(it, P_):
                    qb, kt = it
                    j0 = max(0, kt - 4 * qb)
                    while pending:
                        pending.pop(0)()
                    for j in range(j0, 4):
                        tq = 4 * qb + j
                        for c in range(2):
                            mm(banks[j][:, c * 129:(c + 1) * 129], P_.t[:, c, (j - j0) * 128:(j - j0 + 1) * 128], Va.t[:, kt, 0:129],
                               (kt == 0 and c == 0), kt == tq, [P_.b, Va.buf(kt)], [qbuf[j][0]], inc=(c == 1))
                        if kt == tq:
                            ob = [qbuf[j][0]]
                            sm = small_r.next()
                            issue(DVE, lambda j=j, sm=sm: nc.vector.reciprocal(out=sm.t[:, 0:1], in_=banks[j][:, 128:129]), ob, [sm.b])
                            issue(DVE, lambda j=j, sm=sm: nc.vector.reciprocal(out=sm.t[:, 1:2], in_=banks[j][:, 257:258]), ob, [sm.b])
                            vtt(sm.t[:, 1:2], sm.t[:, 1:2], lam_neg.t[:, l:l + 1], ALU.mult, [sm.b, lam_neg.b], [sm.b])
                            o_ = osb[tq % 2]
                            vts(o_.t[:], banks[j][:, 0:128], sm.t[:, 0:1], ALU.mult, ob + [sm.b], [o_.b])
                            vstt(o_.t[:], banks[j][:, 129:257], sm.t[:, 1:2], o_.t[:], ALU.mult, ALU.add, ob + [sm.b, o_.b], [o_.b])
                            junk = tmpf_r.next()
                            act(junk.t[:, 0:128], o_.t[:], AF.Square, [o_.b], [junk.b, sm.b], accum_out=sm.t[:, 2:3])
                            small_rstd(sm.t[:, 2:3], 128.0, [sm.b], sm)
                            y_ = ybt[tq % 3]
                            vstt(y_.t[:], o_.t[:], sm.t[:, 2:3], zs.t[:, tq, :], ALU.mult, ALU.mult, [o_.b, sm.b, zs.buf(tq)], [y_.b])
                            def fin(j=j, y_=y_, tq=tq):
                                mm(banks[j][:, 384:512], y_.t[:], ident_bf.t[:], True, True, [y_.b, ident_bf.b], [qbuf[j][3]], inc=True)
                                act(Y['b'].t[:, h, tq * 128:(tq + 1) * 128], banks[j][:, 384:512], AF.Copy, [qbuf[j][3]], [Y['b'].buf((h, tq // 4))])
                            pending.append(fin)

                pending = []
                emit_S(iters[0])
                for i_, it in enumerate(iters):
                    P_ = emit_exp(it)
                    if i_ + 1 < len(iters):
                        emit_S(iters[i_ + 1])
                    emit_AV(it, P_)
                while pending:
                    pending.pop(0)()
            barrier()

    def gmlp_phase(s, l):
        pj = Ring([0, 1, 2, 3])
        Y['a'] = sb("yTa", [128, 4, S], BF16, LY["st"])
        with ExitStack() as st:
            wtmp = sb("gm_wtmp", [128, 4, 128], F32, st)
            bsb = sb("gm_bsb", [128, 512], F32, st)
            dma(QS, wtmp.t[:], wsT_d[l].rearrange("g s t -> s g t"), [], [wtmp.b])
            dma(QS, bsb.t[:], bs_d[l:l + 1, :].partition_broadcast(128), [], [bsb.b])
            for gi in range(4):
                issue(POOL, lambda gi=gi: g.affine_select(out=WTg.t[:, gi, :], in_=wtmp.t[:, gi, :], pattern=[[1, 128]], compare_op=ALU.is_ge,
                                                          fill=0.0, base=0, channel_multiplier=-1), [wtmp.b], [WTg.b])
            bi = pj.next()
            for gi in range(4):
                mm(banks[bi][:, gi * 128:(gi + 1) * 128], ones_bf.t[:], WTg.t[:, gi, :], True, True, [ones_bf.b, WTg.b], [qbuf[bi][gi]], inc=True)
            for gi in range(4):
                vstt(Et.t[:, gi, :], banks[bi][:, gi * 128:(gi + 1) * 128], colp.t[:, C_LNB + l * 4 + gi:C_LNB + l * 4 + gi + 1],
                     bsb.t[:, gi * 128:(gi + 1) * 128], ALU.mult, ALU.add, [qbuf[bi][gi], colp.b, bsb.b], [Et.b])
            def uz_block(W, nm, fn, gi, tb):
                sl_ = slice(tb * 512, (tb + 1) * 512)
                bi = pj8.next()
                proj_fm(W, gi * 128, tb, bi)
                yield
                t1 = tmpf_r.next()
                vtt(t1.t[:], banks[bi][:], rstd_bc.t[:, sl_], ALU.mult, bank_bufs(bi) + [rstd_bc.buf(tb)], [t1.b])
                yield
                if nm == "gu":
                    act(Y['a'].t[:, gi, sl_], t1.t[:], fn, [t1.b], [Y['a'].buf((gi, tb))])
                else:
                    t2 = tmpb_r.next()
                    act(t2.t[:], t1.t[:], fn, [t1.b], [t2.b])
                    yield
                    vtt(Y['a'].t[:, gi, sl_], Y['a'].t[:, gi, sl_], t2.t[:], ALU.mult, [Y['a'].buf((gi, tb)), t2.b], [Y['a'].buf((gi, tb))], E=POOL)

            pj8 = Ring([0, 1, 2, 3, 4, 5, 6, 7])
            for nm, fn in (("gu", AF.Gelu_apprx_tanh), ("gz", AF.Silu)):
                W = wget((nm, s, l))
                run_pipelined((uz_block(W, nm, fn, gi, tb) for gi in range(4) for tb in range(NTB)), 3)
            W = wget(("gv", s, l))

            def v_block(t):
                tok = slice(t * 128, (t + 1) * 128)
                bi = pj8.next()
                proj_tm(W, 0, 512, t, banks[bi][:], bank_bufs(bi))
                yield
                v = tmpf_r.next()
                act(v.t[:], banks[bi][:], AF.Gelu_apprx_tanh, bank_bufs(bi) + [rstd_col.b], [v.b], scale=rstd_col.t[:, t:t + 1])
                yield
                sm = small_r.next()
                issue(DVE, lambda v=v, sm=sm: nc.vector.bn_stats(out=sm.t[:, 0:6], in_=v.t[:]), [v.b], [sm.b])
                issue(DVE, lambda sm=sm: nc.vector.bn_aggr(out=sm.t[:, 6:8], in_=sm.t[:, 0:6]), [sm.b], [sm.b])
                yield
                small_rstd(sm.t[:, 7:8], 1.0, [sm.b], sm)
                yield
                vb = tmpb_r.next()
                vts(vb.t[:], v.t[:], sm.t[:, 6:7], ALU.subtract, [v.b, sm.b], [vb.b], s2=sm.t[:, 7:8], op1=ALU.mult)
                yield
                b2 = pj8.next()
                for gi in range(4):
                    mm(banks[b2][:, gi * 128:(gi + 1) * 128], vb.t[:, gi * 128:(gi + 1) * 128], WTg.t[:, gi, :], True, True,
                       [vb.b, WTg.b], bank_bufs(b2), inc=True)
                yield
                mx = tmpf_r.next()
                for gi in range(4):
                    vstt(mx.t[:, gi * 128:(gi + 1) * 128], banks[b2][:, gi * 128:(gi + 1) * 128], colp.t[:, C_LNG + l * 4 + gi:C_LNG + l * 4 + gi + 1], Et.t[:, gi, :],
                         ALU.mult, ALU.add, bank_bufs(b2) + [colp.b, Et.b], [mx.b])
                yield
                for gi in range(4):
                    vtt(Y['a'].t[:, gi, tok], Y['a'].t[:, gi, tok], mx.t[:, gi * 128:(gi + 1) * 128], ALU.mult, [Y['a'].buf((gi, t // 4)), mx.b], [Y['a'].buf((gi, t // 4))], E=POOL)

            run_pipelined((v_block(t) for t in range(NT)), 2)
            barrier()

    def merge_tail_phase(s, l):
        pj = Ring([0, 1, 2, 3])
        last = (l == nl - 1)
        with ExitStack() as st:
            mg = sb("mg", [128, KC, S], BF16, st)
            x3 = [sb(f"x3_{i}", [128, 512], F32, st) for i in range(3)]
            x3_r = Ring(xin + x3)
            PD = 2

            class Prefetch:
                def __init__(self, order, srcfn, depfn):
                    self.order, self.srcfn, self.depfn = order, srcfn, depfn
                    self.nxt = 0
                    self.tiles = {}

                def get(self, i):
                    while self.nxt <= min(i + PD, len(self.order) - 1):
                        j_, tb_ = self.order[self.nxt]
                        xt_ = x3_r.next()
                        dma(QS, xt_.t[:], self.srcfn(j_, tb_), self.depfn(j_, tb_), [xt_.b])
                        self.tiles[self.nxt] = xt_
                        self.nxt += 1
                    return self.tiles.pop(i)
            pTb = sb("pTb", [128, 2, S], BF16, st)
            rstd2 = rstd_bc
            dma(QW, pTb.t[:], pT_d[l, s].rearrange("(k p) t -> p k t", p=128), [], [pTb.b])
            for j in range(8):
                Wg = wget(("mg", s, l, j))
                Wb = wget(("mb", s, l, j))
                for tb in range(NTB):
                    sl_ = slice(tb * 512, (tb + 1) * 512)
                    acc = None
                    for br in range(3):
                        bg = pj.next()
                        proj_fm(Wg, br * 128, tb, bg)
                        gt = tmpf_r.next()
                        vtt(gt.t[:], banks[bg][:], rstd_bc.t[:, sl_], ALU.mult, bank_bufs(bg) + [rstd_bc.buf(tb)], [gt.b])
                        act(gt.t[:], gt.t[:], AF.Sigmoid, [gt.b], [gt.b])
                        by = pj.next()
                        for kc in range(4):
                            mm(banks[by][:], Wb.t[:, kc, br * 128:(br + 1) * 128], Y['abc'[br]].t[:, kc, sl_], kc == 0, kc == 3,
                               [Wb.b, Y['abc'[br]].buf((kc, tb))], bank_bufs(by), inc=(kc == 3))
                        if br == 0:
                            acc = tmpf_r.next()
                            vtt(acc.t[:], banks[by][:], gt.t[:], ALU.mult, bank_bufs(by) + [gt.b], [acc.b])
                        else:
                            vtt(gt.t[:], banks[by][:], gt.t[:], ALU.mult, bank_bufs(by) + [gt.b], [gt.b])
                            if br == 1:
                                vtt(acc.t[:], acc.t[:], gt.t[:], ALU.add, [acc.b, gt.b], [acc.b], E=POOL)
                            else:
                                vtt(mg.t[:, j, sl_], acc.t[:], gt.t[:], ALU.add, [acc.b, gt.b], [mg.buf(tb)], E=POOL)
            src = xT_d[s] if l == 0 else xs_d[s]
            xsb = [[Buf(f"xs{j}_{tb}") for tb in range(NTB)] for j in range(8)]
            order = [(j, tb) for j in range(8) for tb in range(NTB)]
            pf1 = Prefetch(order, lambda j_, tb_: src[j_ * 128:(j_ + 1) * 128, tb_ * 512:(tb_ + 1) * 512], lambda j_, tb_: [])
            def t1_block(W, j, tb):
                sl_ = slice(tb * 512, (tb + 1) * 512)
                bi = pj.next()
                for kc in range(KC):
                    mm(banks[bi][:], W.t[:, kc, 0:128], mg.t[:, kc, sl_], kc == 0, kc == KC - 1, [W.b, mg.buf(tb)], bank_bufs(bi), inc=(kc == KC - 1))
                xt = pf1.get(j * NTB + tb)
                yield
                vtt(xt.t[:], banks[bi][:], xt.t[:], ALU.add, bank_bufs(bi) + [xt.b], [xt.b])
                yield
                dma(QS, xs_d[s][j * 128:(j + 1) * 128, sl_], xt.t[:], [xt.b], [xsb[j][tb]])
                act(xb.t[:, j, sl_], xt.t[:], AF.Copy, [xt.b, colp.b], [xb.buf(tb)], scale=colp.t[:, C_PG + l * 8 + j:C_PG + l * 8 + j + 1])
                sq = tmpb_r.next()
                act(sq.t[:], xt.t[:], AF.Square, [xt.b], [sq.b])
                yield
                mm(banks[4 + tb][:], ones_bf.t[:], sq.t[:], j == 0, j == 7, [ones_bf.b, sq.b], bank_bufs(4 + tb), inc=True)

            for j in range(8):
                W = wget(("wo", s, l, j))
                run_pipelined((t1_block(W, j, tb) for tb in range(NTB)), 3)
            for tb in range(NTB):
                rstd_from_ps(rstd2.t[:, tb * 512:(tb + 1) * 512], banks[4 + tb][:], float(D), bank_bufs(4 + tb), [rstd2.buf(tb)])
            if debug and s == 0 and l == 0:
                for bi_, nm_ in enumerate('abc'):
                    dma(QS, dbg_d[:, bi_ * 4:(bi_ + 1) * 4, :], Y[nm_].t[:], [Y[nm_].buf((c, tb)) for c in range(4) for tb in range(NTB)], [])
                dma(QS, dbgx_d[0], xs_d[0], [xsb[j][tb] for j in range(8) for tb in range(NTB)], [])
            pf3 = Prefetch(order, lambda j_, tb_: xs_d[s][j_ * 128:(j_ + 1) * 128, tb_ * 512:(tb_ + 1) * 512], lambda j_, tb_: [xsb[j_][tb_]])
            pj3 = Ring([0, 1, 2, 3, 4, 5, 6, 7])

            def t3_block(W, j, tb):
                sl_ = slice(tb * 512, (tb + 1) * 512)
                bg = pj3.next()
                proj_fm(W, 0, tb, bg)
                xt = pf3.get(j * NTB + tb)
                yield
                gt = tmpf_r.next()
                vtt(gt.t[:], banks[bg][:], rstd2.t[:, sl_], ALU.mult, bank_bufs(bg) + [rstd2.buf(tb)], [gt.b])
                yield
                act(gt.t[:], gt.t[:], AF.Sigmoid, [gt.b], [gt.b])
                bp = pj3.next()
                for kc in range(2):
                    mm(banks[bp][:], W.t[:, kc, 128:256], pTb.t[:, kc, sl_], kc == 0, kc == 1, [W.b, pTb.b], bank_bufs(bp), inc=(kc == 1))
                yield
                vtt(gt.t[:], banks[bp][:], gt.t[:], ALU.mult, bank_bufs(bp) + [gt.b], [gt.b])
                yield
                vtt(xt.t[:], xt.t[:], gt.t[:], ALU.add, [xt.b, gt.b], [xt.b], E=POOL)
                yield
                dma(QS, xs_d[s][j * 128:(j + 1) * 128, sl_], xt.t[:], [xt.b], [xsb[j][tb]])

            for j in range(8):
                W = wget(("pg", s, l, j))
                run_pipelined((t3_block(W, j, tb) for tb in range(NTB)), 3)
            if debug and s == 0 and l == 0:
                dma(QS, dbgx_d[1], xs_d[0], [xsb[j][tb] for j in range(8) for tb in range(NTB)], [])
            barrier()
            if last:
                order2 = [(j, tb) for tb in range(NTB) for j in range(KC)]
                pfa = Prefetch(order2, lambda j_, tb_: xs_d[s][j_ * 128:(j_ + 1) * 128, tb_ * 512:(tb_ + 1) * 512], lambda j_, tb_: [xsb[j_][tb_]])
                for tb in range(NTB):
                    bi = 4 + tb
                    for j in range(KC):
                        xt = pfa.get(tb * KC + j)
                        sq = tmpb_r.next()
                        act(sq.t[:], xt.t[:], AF.Square, [xt.b], [sq.b])
                        mm(banks[bi][:], ones_bf.t[:], sq.t[:], j == 0, j == KC - 1, [ones_bf.b, sq.b], bank_bufs(bi), inc=True)
                    rstd_from_ps(rstd2.t[:, tb * 512:(tb + 1) * 512], banks[bi][:], float(D), bank_bufs(bi), [rstd2.buf(tb)])
                pfb = Prefetch(order2, lambda j_, tb_: xs_d[s][j_ * 128:(j_ + 1) * 128, tb_ * 512:(tb_ + 1) * 512], lambda j_, tb_: [xsb[j_][tb_]])
                for tb in range(NTB):
                    sl_ = slice(tb * 512, (tb + 1) * 512)
                    for j in range(KC):
                        xt = pfb.get(tb * KC + j)
                        vstt(xt.t[:], xt.t[:], colp.t[:, C_FG + j:C_FG + j + 1], rstd2.t[:, sl_], ALU.mult, ALU.mult, [xt.b, colp.b, rstd2.buf(tb)], [xt.b])
                        OUT_TOKS.append(dma(QS, yT_d[s][j * 128:(j + 1) * 128, sl_], xt.t[:], [xt.b], []))
                barrier()

    OUT_TOKS = []
    for s in range(nseq):
        if upto >= 1:
            rotary_tables(s)
        for l in range(nl):
            LY["st"] = ExitStack()
            if upto >= 2:
                p0(s, l, Ring([0, 1, 2, 3]))
            if upto >= 3:
                deltanet_phase(s, l)
            if upto >= 4:
                attention_phase(s, l)
            if upto >= 5:
                gmlp_phase(s, l)
            if upto >= 6:
                merge_tail_phase(s, l)
            barrier()
            LY["st"].close()
    if upto < 6 and debug:
        pass
    for sl in QS.slots:
        if sl.count > 0:
            SP.wait((sl.sem, sl.count, sl))
    for sl in QW.slots:
        if sl.count > 0:
            POOL.wait((sl.sem, sl.count, sl))
    for e in (PE, ACT, DVE, POOL):
        if e.count > 0:
            SP.wait(e.cur_tok())
    es.close()
    return nc


def make_in_maps(inputs, ncores=8):
    f = lambda a: np.ascontiguousarray(np.asarray(a, dtype=np.float32))
    x = f(inputs["x"])
    p = f(inputs["p"])
    pos = np.ascontiguousarray(np.asarray(inputs["positions"], dtype=np.int32))
    colp = np.zeros((128, NCOL), np.float32)
    colp[:, C_NG:C_NG + 32] = f(inputs["norm_g"]).reshape(L, 8, 128).transpose(2, 0, 1).reshape(128, 32)
    colp[:, C_PG:C_PG + 32] = f(inputs["ple_norm_g"]).reshape(L, 8, 128).transpose(2, 0, 1).reshape(128, 32)
    colp[:, C_FG:C_FG + 8] = f(inputs["final_norm_g"]).reshape(8, 128).T
    colp[:, C_LNG:C_LNG + 16] = f(inputs["gm_ln_g"]).reshape(L, 4, 128).transpose(2, 0, 1).reshape(128, 16)
    colp[:, C_LNB:C_LNB + 16] = f(inputs["gm_ln_b"]).reshape(L, 4, 128).transpose(2, 0, 1).reshape(128, 16)
    colp[:, C_DNG:C_DNG + 4] = f(inputs["dn_norm_g"]).T
    colp[:, C_CONV:C_CONV + 192] = f(inputs["dn_conv_w"]).reshape(L, 4, 12, 128).transpose(3, 0, 1, 2).reshape(128, 192)
    rowp = np.zeros((1, NROW), np.float32)
    rowp[0, R_SUBLN:R_SUBLN + 512] = f(inputs["da_subln_g"]).reshape(-1)
    rowp[0, R_DNG:R_DNG + 512] = f(inputs["dn_norm_g"]).reshape(-1)
    rowp[0, R_ALOG:R_ALOG + 16] = f(inputs["dn_a_log"]).reshape(-1)
    rowp[0, R_DTB:R_DTB + 16] = f(inputs["dn_dt_bias"]).reshape(-1)
    lamp = np.stack([f(inputs[k]).reshape(-1) for k in ("da_lq1", "da_lk1", "da_lq2", "da_lk2")], 0)
    w_br = np.ascontiguousarray(np.stack([f(inputs["w_br_a"]), f(inputs["w_br_b"]), f(inputs["w_br_c"])], 1))
    shared = {
        "w_in": f(inputs["w_in"]), "w_br": w_br, "w_out": f(inputs["w_out"]), "w_pg": f(inputs["w_ple_gate"]),
        "w_pp": f(inputs["w_ple_proj"]), "gm_wsT": np.ascontiguousarray(f(inputs["gm_ws"]).transpose(0, 1, 3, 2)),
        "gm_bs": f(inputs["gm_bs"]).reshape(L, 512), "gm_lnb": f(inputs["gm_ln_b"]).reshape(L, 512),
        "colp": colp, "rowp": rowp, "lamp": np.ascontiguousarray(lamp),
    }
    maps = []
    for c in range(ncores):
        b0 = c * NSEQ
        m = dict(shared)
        m["xT"] = np.ascontiguousarray(x[b0:b0 + NSEQ].transpose(0, 2, 1))
        m["pT"] = np.ascontiguousarray(p[:, b0:b0 + NSEQ].transpose(0, 1, 3, 2))
        m["pos"] = np.ascontiguousarray(pos[b0:b0 + NSEQ])
        maps.append(m)
    return maps


def kernel(**inputs):
    nc = build()
    maps = make_in_maps(inputs, 8)
    res = run_bass_kernel_spmd(nc, maps, core_ids=list(range(8)))
    outs = [np.asarray(r["yT"]).transpose(0, 2, 1) for r in res.results]
    return np.ascontiguousarray(np.concatenate(outs, axis=0).astype(np.float32))
```

```python
import math
from contextlib import ExitStack
import numpy as np
import concourse.bass as bass
import concourse.mybir as mybir
from concourse.bass_utils import run_bass_kernel_spmd

F32 = mybir.dt.float32
BF16 = mybir.dt.bfloat16
I32 = mybir.dt.int32
AF = mybir.ActivationFunctionType
ALU = mybir.AluOpType
AX = mybir.AxisListType

D = 1024
S = 2048
L = 4
NSEQ = 2
NT = 16
NTB = 4
KC = 8
EPS = 1e-6
N_IN = 8712
GM_U, GM_V, GM_Z = 0, 512, 1024
DA_Q, DA_K, DA_V, DA_Z = 1536, 2048, 2560, 3072
DN_Q, DN_K, DN_V, DN_A, DN_B, DN_Z = 3584, 4096, 4608, 5120, 5124, 5128
GATES = 5640
ROPE_THETA = 500000.0
NEG = -30000.0

C_NG, C_PG, C_FG, C_LNG, C_LNB, C_CONV = 0, 32, 64, 72, 88, 104
NCOL = 104 + 192 + 4
C_DNG = 104 + 192
R_SUBLN, R_DNG, R_ALOG, R_DTB = 0, 512, 1024, 1040
NROW = 1056


class Buf:
    __slots__ = ("w", "r", "name", "excl")

    def __init__(self, name="", excl=False):
        self.w = None
        self.r = {}
        self.name = name
        self.excl = excl


class Eng:
    MAXC = 30000

    def __init__(self, nc, e, name, is_pe=False):
        self.nc, self.e, self.name, self.is_pe = nc, e, name, is_pe
        self.nsem = 0
        self.new_sem()
        self.waited = {}
        self.pending = False

    def new_sem(self):
        self.sem = self.nc.alloc_semaphore(name=f"s_{self.name}_{self.nsem}")
        self.nsem += 1
        self.count = 0

    def wait(self, tok):
        sem, val, owner = tok[0], tok[1], tok[2]
        if owner is self and self.is_pe:
            return
        k = id(sem)
        if self.waited.get(k, 0) >= val:
            return
        self.e.wait_ge(sem, val)
        self.waited[k] = val

    def needed(self, toks):
        best = {}
        for tok in toks:
            sem, val, owner = tok[0], tok[1], tok[2]
            snap = tok[3] if len(tok) > 3 else None
            if owner is self and self.is_pe:
                continue
            k = id(sem)
            if self.waited.get(k, 0) >= val:
                continue
            if k not in best or best[k][1] < val:
                best[k] = (sem, val, snap)
        items = list(best.values())
        out = []
        for i, (sem, val, snap) in enumerate(items):
            k = id(sem)
            covered = False
            for j, (s2, v2, snap2) in enumerate(items):
                if j != i and snap2 is not None and snap2.get(k, 0) >= val:
                    covered = True
                    break
            if not covered:
                out.append((sem, val))
        for sem, val, snap in items:
            k = id(sem)
            if self.waited.get(k, 0) < val:
                self.waited[k] = val
            if snap is not None:
                for k2, v2 in snap.items():
                    if self.waited.get(k2, 0) < v2:
                        self.waited[k2] = v2
        return out

    def snap(self):
        return dict(self.waited)

    def cur_tok(self):
        return (self.sem, self.count, self)


class Slot:
    def __init__(self, nc, name):
        self.sem = nc.alloc_semaphore(name=name)
        self.count = 0


class Queue:
    def __init__(self, nc, eng, name, nslots):
        self.eng = eng
        self.slots = [Slot(nc, f"q_{name}_{i}") for i in range(nslots)]
        self.rr = 0


def _deps(R, W):
    deps = []
    for b in R:
        if b.w is not None:
            deps.append(b.w)
    for b in W:
        if b.w is not None:
            deps.append(b.w)
        deps.extend(b.r.values())
    return deps


def _update(tok, R, W):
    owner = tok[2]
    for b in R:
        b.r[id(owner)] = tok
    for b in W:
        b.w = tok
        b.r = {}


def issue(E, fn, R=(), W=(), inc=True):
    if any(b.excl for b in R):
        W = list(W) + [b for b in R if b.excl]
        R = [b for b in R if not b.excl]
    if E.count >= Eng.MAXC and not E.pending:
        E.new_sem()
    nd = E.needed(_deps(R, W))
    for sem, val in nd[:-1]:
        E.e.wait_ge(sem, val)
    ins = fn()
    if nd:
        ins.wait_op(nd[-1][0], nd[-1][1], "sem-ge")
    if inc:
        E.count += 1
        ins.then_inc(E.sem, 1)
        E.pending = False
        tok = (E.sem, E.count, E, E.snap())
    else:
        assert E.is_pe
        E.pending = True
        tok = (E.sem, E.count + 1, E, E.snap())
    _update(tok, R, W)
    return tok


def dma(Q, out, in_, R=(), W=()):
    E = Q.eng
    slot = Q.slots[Q.rr]
    Q.rr = (Q.rr + 1) % len(Q.slots)
    if slot.count > 0:
        E.wait((slot.sem, slot.count, slot))
    for tok in _deps(R, W):
        E.wait(tok)
    E.e.dma_start(out=out, in_=in_).then_inc(slot.sem, 16)
    slot.count += 16
    tok = (slot.sem, slot.count, slot, E.snap())
    _update(tok, R, W)
    return tok


class TT:
    def __init__(self, t, name):
        self.t = t
        self.b = Buf(name)
        self.rb = {}
        self.name = name

    def buf(self, key):
        if key not in self.rb:
            self.rb[key] = Buf(f"{self.name}{key}")
        return self.rb[key]


def build(nl=L, nseq=NSEQ, debug=False, upto=6):
    nc = bass.Bass("TRN2", target_bir_lowering=False)
    dt = nc.dram_tensor
    xT_d = dt("xT", [NSEQ, D, S], F32, kind="ExternalInput").ap()
    pT_d = dt("pT", [L, NSEQ, 256, S], F32, kind="ExternalInput").ap()
    pos_d = dt("pos", [NSEQ, S], I32, kind="ExternalInput").ap()
    w_in_d = dt("w_in", [L, D, N_IN], F32, kind="ExternalInput").ap()
    w_br_d = dt("w_br", [L, 3, 512, D], F32, kind="ExternalInput").ap()
    w_out_d = dt("w_out", [L, D, D], F32, kind="ExternalInput").ap()
    w_pg_d = dt("w_pg", [L, D, D], F32, kind="ExternalInput").ap()
    w_pp_d = dt("w_pp", [L, 256, D], F32, kind="ExternalInput").ap()
    wsT_d = dt("gm_wsT", [L, 4, 128, 128], F32, kind="ExternalInput").ap()
    bs_d = dt("gm_bs", [L, 512], F32, kind="ExternalInput").ap()
    lnb_d = dt("gm_lnb", [L, 512], F32, kind="ExternalInput").ap()
    colp_d = dt("colp", [128, NCOL], F32, kind="ExternalInput").ap()
    rowp_d = dt("rowp", [1, NROW], F32, kind="ExternalInput").ap()
    lamp_d = dt("lamp", [4, L * 64], F32, kind="ExternalInput").ap()
    xs_d = dt("xs", [NSEQ, D, S], F32, kind="Internal").ap()
    yT_d = dt("yT", [NSEQ, D, S], F32, kind="ExternalOutput").ap()
    dbg_d = None
    if debug:
        dbg_d = dt("dbg", [128, 12, S], BF16, kind="ExternalOutput").ap()
        dbgx_d = dt("dbgx", [2, D, S], F32, kind="ExternalOutput").ap()

    es = ExitStack()
    PE = Eng(nc, nc.tensor, "pe", is_pe=True)
    ACT = Eng(nc, nc.scalar, "act")
    DVE = Eng(nc, nc.vector, "dve")
    POOL = Eng(nc, nc.gpsimd, "pool")
    SP = Eng(nc, nc.sync, "sp")
    QS = Queue(nc, SP, "sp", 12)
    QW = Queue(nc, POOL, "pw", 8)

    uniq = {"n": 0}

    def sb(name, shape, dtype, stack=es):
        uniq["n"] += 1
        return TT(stack.enter_context(nc.sbuf_tensor(f"sb{uniq['n']}_{name}", shape, dtype)), name)

    banks = [es.enter_context(nc.psum_tensor(f"bank{i}", [128, 512], F32)) for i in range(8)]
    qbuf = []
    for i in range(8):
        _b = Buf(f"ps{i}", excl=True)
        qbuf.append([_b, _b, _b, _b])

    def bank_bufs(i):
        return [qbuf[i][0]]

    class Ring:
        def __init__(self, items):
            self.items = items
            self.i = 0

        def next(self):
            it = self.items[self.i]
            self.i = (self.i + 1) % len(self.items)
            return it

    def act(out, in_, func, R, W, bias=None, scale=None, accum_out=None):
        kw = {}
        if bias is not None:
            kw["bias"] = bias
        if scale is not None:
            kw["scale"] = scale
        if accum_out is not None:
            kw["accum_out"] = accum_out
        return issue(ACT, lambda: nc.scalar.activation(out=out, in_=in_, func=func, **kw), R, W)

    import os as _os
    NOPOOL = _os.environ.get("NOPOOL", "0") == "1"

    def vtt(out, in0, in1, op, R, W, E=None):
        E = E or DVE
        if NOPOOL:
            E = DVE
        return issue(E, lambda: E.e.tensor_tensor(out=out, in0=in0, in1=in1, op=op), R, W)

    def vts(out, in0, s1, op0, R, W, s2=None, op1=None, E=None):
        E = E or DVE
        if NOPOOL:
            E = DVE
        if op1 is None:
            return issue(E, lambda: E.e.tensor_scalar(out=out, in0=in0, scalar1=s1, scalar2=None, op0=op0), R, W)
        return issue(E, lambda: E.e.tensor_scalar(out=out, in0=in0, scalar1=s1, scalar2=s2, op0=op0, op1=op1), R, W)

    def vstt(out, in0, scalar, in1, op0, op1, R, W):
        return issue(DVE, lambda: nc.vector.scalar_tensor_tensor(out=out, in0=in0, scalar=scalar, in1=in1, op0=op0, op1=op1), R, W)

    def vcopy(out, in_, R, W, E=None):
        E = E or DVE
        return issue(E, lambda: E.e.tensor_copy(out=out, in_=in_), R, W)

    def mm(out, lhsT, rhs, start, stop, R, W, inc):
        return issue(PE, lambda: nc.tensor.matmul(out, lhsT=lhsT, rhs=rhs, start=start, stop=stop, skip_group_check=True), R, W, inc=inc)

    def barrier():
        engs = [PE, ACT, DVE, POOL, SP]
        assert not PE.pending
        toks = [e.cur_tok() for e in engs if e.count > 0]
        for q in (QS,):
            for sl in q.slots:
                if sl.count > 0:
                    toks.append((sl.sem, sl.count, sl))
        for e in (PE, ACT, DVE, SP, POOL):
            for tk in toks:
                if tk[2] is not e:
                    e.wait(tk)

    ident_bf = sb("ident_bf", [128, 128], BF16)
    ones_bf = sb("ones_bf", [128, 128], BF16)
    U_bf = sb("U_bf", [128, 128], BF16)
    Y_bf = sb("Y_bf", [128, 128], BF16)
    MnS_bf = sb("MnS_bf", [128, 128], BF16)
    MnST_bf = sb("MnST_bf", [128, 128], BF16)
    MnIT_bf = sb("MnIT_bf", [128, 128], BF16)
    Psw_bf = sb("Psw_bf", [128, 128], BF16)
    U_f = sb("U_f", [128, 128], F32)
    ones_f = sb("ones_f", [128, 128], F32)
    zeros_f = sb("zeros_f", [128, 128], F32)
    e0_f = sb("e0_f", [128, 1], F32)
    cc = sb("cc", [128, 8], F32)
    colp = sb("colp", [128, NCOL], F32)
    rowb = sb("rowb", [128, NROW], F32)
    lam_neg = sb("lam_neg", [128, L], F32)
    negealog = sb("negealog", [128, L * 4], F32)
    rotc = sb("rotc", [128, 4], F32)
    CONSTS = [ident_bf.b, ones_bf.b, U_bf.b, Y_bf.b, MnS_bf.b, MnST_bf.b, MnIT_bf.b, Psw_bf.b, U_f.b, ones_f.b,
              zeros_f.b, e0_f.b, cc.b, colp.b, rowb.b, lam_neg.b, negealog.b, rotc.b]

    g = nc.gpsimd

    def pool(fn, R, W):
        return issue(POOL, fn, R, W)

    pool(lambda: g.memset(ones_f.t[:], 1.0), [], [ones_f.b])
    pool(lambda: g.memset(zeros_f.t[:], 0.0), [], [zeros_f.b])
    pool(lambda: g.memset(ones_bf.t[:], 1.0), [], [ones_bf.b])
    pool(lambda: g.memset(e0_f.t[:], 0.0), [], [e0_f.b])
    pool(lambda: g.memset(e0_f.t[0:1, :], 1.0), [], [e0_f.b])
    pool(lambda: g.memset(cc.t[:, 0:1], EPS), [], [cc.b])
    pool(lambda: g.memset(cc.t[:, 1:2], 1.0), [], [cc.b])
    pool(lambda: g.memset(cc.t[:, 2:3], math.log(128.0 ** -0.5)), [], [cc.b])
    pool(lambda: g.memset(cc.t[:, 3:4], 0.0), [], [cc.b])

    def affsel(out_t, in_t, pat, cmp, fill, base, cm):
        pool(lambda: g.affine_select(out=out_t.t[:], in_=in_t.t[:], pattern=pat, compare_op=cmp, fill=fill, base=base,
                                     channel_multiplier=cm), [in_t.b], [out_t.b])

    affsel(ident_bf, ones_f, [[1, 128]], ALU.is_equal, 0.0, 0, -1)
    affsel(U_bf, ones_f, [[1, 128]], ALU.is_ge, 0.0, 0, -1)
    affsel(U_f, ones_f, [[1, 128]], ALU.is_ge, 0.0, 0, -1)
    affsel(Y_bf, ones_f, [[-1, 128]], ALU.is_gt, 0.0, 0, 1)
    affsel(MnS_bf, zeros_f, [[-1, 128]], ALU.is_gt, NEG, 0, 1)
    affsel(MnST_bf, zeros_f, [[1, 128]], ALU.is_gt, NEG, 0, -1)
    affsel(MnIT_bf, zeros_f, [[1, 128]], ALU.is_ge, NEG, 0, -1)
    with ExitStack() as st:
        bA = sb("bA", [128, 128], BF16, st)
        bB = sb("bB", [128, 128], BF16, st)
        affsel(bA, ones_f, [[1, 128]], ALU.is_equal, 0.0, 8, -1)
        affsel(bB, ones_f, [[1, 128]], ALU.is_equal, 0.0, -8, -1)
        pool(lambda: g.memset(Psw_bf.t[:], 0.0), [], [Psw_bf.b])
        for c0 in (0, 64):
            pool(lambda c0=c0: g.tensor_copy(out=Psw_bf.t[:, c0:c0 + 8], in_=bA.t[:, c0:c0 + 8]), [bA.b], [Psw_bf.b])
            pool(lambda c0=c0: g.tensor_copy(out=Psw_bf.t[:, c0 + 8:c0 + 16], in_=bB.t[:, c0 + 8:c0 + 16]), [bB.b], [Psw_bf.b])
        pidx = sb("pidx", [128, 1], I32, st)
        pt_i = sb("pt_i", [128, 2], I32, st)
        pt_f = sb("pt_f", [128, 4], F32, st)
        pool(lambda: g.iota(pidx.t[:], pattern=[[0, 1]], base=0, channel_multiplier=1), [], [pidx.b])
        issue(DVE, lambda: nc.vector.tensor_single_scalar(out=pt_i.t[:, 0:1], in_=pidx.t[:], scalar=63, op=ALU.bitwise_and), [pidx.b], [pt_i.b])
        issue(DVE, lambda: nc.vector.tensor_single_scalar(out=pt_i.t[:, 1:2], in_=pidx.t[:], scalar=7, op=ALU.bitwise_and), [pidx.b], [pt_i.b])
        vcopy(pt_f.t[:, 0:2], pt_i.t[:, 0:2], [pt_i.b], [pt_f.b])
        act(pt_f.t[:, 2:3], pt_f.t[:, 1:2], AF.Exp, [pt_f.b], [pt_f.b], scale=-math.log(ROPE_THETA) / 8.0)
        vts(rotc.t[:, 0:1], pt_f.t[:, 2:3], 1.0 / (2.0 * math.pi), ALU.mult, [pt_f.b], [rotc.b])
        vts(rotc.t[:, 2:3], pt_f.t[:, 0:1], 16.0, ALU.is_lt, [pt_f.b], [rotc.b])
        vts(pt_f.t[:, 3:4], pt_f.t[:, 0:1], 8.0, ALU.is_ge, [pt_f.b], [pt_f.b], s2=2.0, op1=ALU.mult)
        vstt(rotc.t[:, 1:2], pt_f.t[:, 3:4], -1.0, rotc.t[:, 2:3], ALU.add, ALU.mult, [pt_f.b, rotc.b], [rotc.b])
        vts(rotc.t[:, 3:4], rotc.t[:, 2:3], -1.0, ALU.mult, [rotc.b], [rotc.b], s2=1.0, op1=ALU.add)
        dma(QS, colp.t[:], colp_d, [], [colp.b])
        dma(QS, rowb.t[:], rowp_d.partition_broadcast(128), [], [rowb.b])
        lamb = sb("lamb", [128, 4, L * 64], F32, st)
        for i in range(4):
            dma(QS, lamb.t[:, i, :], lamp_d[i:i + 1, :].partition_broadcast(128), [], [lamb.b])
        lprod = sb("lprod", [128, 2, L * 64], F32, st)
        lsum = sb("lsum", [128, 2 * L], F32, st)
        vtt(lprod.t[:, 0, :], lamb.t[:, 0, :], lamb.t[:, 1, :], ALU.mult, [lamb.b], [lprod.b])
        vtt(lprod.t[:, 1, :], lamb.t[:, 2, :], lamb.t[:, 3, :], ALU.mult, [lamb.b], [lprod.b])
        for i in range(2):
            for l in range(L):
                issue(DVE, lambda i=i, l=l: nc.vector.tensor_reduce(out=lsum.t[:, i * L + l:i * L + l + 1], in_=lprod.t[:, i, l * 64:(l + 1) * 64],
                                                                     axis=AX.X, op=ALU.add), [lprod.b], [lsum.b])
        act(lsum.t[:], lsum.t[:], AF.Exp, [lsum.b], [lsum.b])
        for l in range(L):
            li = 0.8 - 0.6 * math.exp(-0.3 * l)
            vstt(lam_neg.t[:, l:l + 1], lsum.t[:, L + l:L + l + 1], -li, lsum.t[:, l:l + 1], ALU.add, ALU.subtract, [lsum.b], [lam_neg.b])
        act(negealog.t[:], rowb.t[:, R_ALOG:R_ALOG + 16], AF.Exp, [rowb.b], [negealog.b])
        vts(negealog.t[:], negealog.t[:], -1.0, ALU.mult, [negealog.b], [negealog.b])
        barrier()

    xb = sb("xb", [128, KC, S], BF16)
    rstd_bc = sb("rstd_bc", [128, S], F32)
    rstd_col = sb("rstd_col", [128, NT], F32)
    Y = {}
    LY = {"st": None}
    COS = sb("COS", [128, S], BF16)
    SIN = sb("SIN", [128, S], BF16)
    NSLOT = 4
    slots = [sb(f"wslot{i}", [128, KC, 512], BF16) for i in range(NSLOT)]
    xin = [sb(f"xin{i}", [128, 512], F32) for i in range(3)]
    xin_r = Ring(xin)
    tmpf = [sb(f"tmpf{i}", [128, 512], F32) for i in range(4)]
    tmpf_r = Ring(tmpf)
    tmpb = [sb(f"tmpb{i}", [128, 512], BF16) for i in range(3)]
    tmpb_r = Ring(tmpb)
    small = [sb(f"small{i}", [128, 8], F32) for i in range(6)]
    small_r = Ring(small)
    Et = sb("Et", [128, 4, 128], F32)
    WTg = sb("WTg", [128, 4, 128], BF16)

    def wsrc(ap2d):
        return ap2d.rearrange("(k p) n -> p k n", p=128)

    def plan():
        P = []
        for s in range(nseq):
            for l in range(nl):
                W = w_in_d[l]
                P.append((("ab", s, l), [((0, KC, 0, 8), wsrc(W[:, DN_A:DN_A + 8]))]))
                for h in range(4):
                    P.append((("dn", s, l, h), [((0, KC, i * 128, 128), wsrc(W[:, c + h * 128:c + (h + 1) * 128]))
                                                for i, c in enumerate((DN_Q, DN_K, DN_V, DN_Z))]))
                for h in range(4):
                    P.append((("at", s, l, h), [((0, KC, i * 128, 128), wsrc(W[:, c + h * 128:c + (h + 1) * 128]))
                                                for i, c in enumerate((DA_Q, DA_K, DA_V, DA_Z))]))
                for nm, c in (("gu", GM_U), ("gz", GM_Z), ("gv", GM_V)):
                    P.append(((nm, s, l), [((0, KC, 0, 512), wsrc(W[:, c:c + 512]))]))
                for j in range(8):
                    P.append((("mg", s, l, j), [((0, KC, i * 128, 128), wsrc(W[:, GATES + i * 1024 + j * 128:GATES + i * 1024 + (j + 1) * 128]))
                                                for i in range(3)]))
                    P.append((("mb", s, l, j), [((0, 4, i * 128, 128), wsrc(w_br_d[l, i][:, j * 128:(j + 1) * 128])) for i in range(3)]))
                for j in range(8):
                    P.append((("wo", s, l, j), [((0, KC, 0, 128), wsrc(w_out_d[l][:, j * 128:(j + 1) * 128]))]))
                for j in range(8):
                    P.append((("pg", s, l, j), [((0, KC, 0, 128), wsrc(w_pg_d[l][:, j * 128:(j + 1) * 128])),
                                                ((0, 2, 128, 128), wsrc(w_pp_d[l][:, j * 128:(j + 1) * 128]))]))
        lvl = {'ab': 3, 'dn': 3, 'at': 4, 'gu': 5, 'gz': 5, 'gv': 5, 'mg': 6, 'mb': 6, 'wo': 6, 'pg': 6}
        return [e for e in P if lvl[e[0][0]] <= upto]

    PLAN = plan()
    wstate = {"emit": 0, "use": 0}

    def wget(key):
        k = wstate["use"]
        assert PLAN[k][0] == key, (PLAN[k][0], key)
        while wstate["emit"] < min(len(PLAN), k + NSLOT - 1):
            i = wstate["emit"]
            sl = slots[i % NSLOT]
            for (k0, k1, c0, n), src in PLAN[i][1]:
                dma(QW, sl.t[:, k0:k1, c0:c0 + n], src, [], [sl.b])
            wstate["emit"] += 1
        wstate["use"] += 1
        return slots[k % NSLOT]

    def run_pipelined(gen_iter, width):
        active = []
        it = iter(gen_iter)
        done = False
        while True:
            while len(active) < width and not done:
                try:
                    active.append(next(it))
                except StopIteration:
                    done = True
            if not active:
                break
            for g_ in list(active):
                try:
                    next(g_)
                except StopIteration:
                    active.remove(g_)

    def rstd_from_ps(out_ap, ps_ap, n_div, R, W, bias_extra=None, inplace=False):
        shp = ps_ap.shape[-1]
        if inplace:
            act(out_ap, ps_ap, AF.Ln, R, W, bias=cc.t[:, 0:1], scale=1.0 / n_div)
            if bias_extra is None:
                act(out_ap, out_ap, AF.Exp, W, W, scale=-0.5)
            else:
                act(out_ap, out_ap, AF.Exp, W, W, scale=-0.5, bias=bias_extra)
            return
        t = tmpf_r.next()
        act(t.t[:, 0:shp], ps_ap, AF.Ln, R, [t.b], bias=cc.t[:, 0:1], scale=1.0 / n_div)
        if bias_extra is None:
            act(out_ap, t.t[:, 0:shp], AF.Exp, [t.b], W, scale=-0.5)
        else:
            act(out_ap, t.t[:, 0:shp], AF.Exp, [t.b], W, scale=-0.5, bias=bias_extra)

    def small_rstd(ss_ap, n_div, R, sm):
        act(ss_ap, ss_ap, AF.Ln, R, [sm.b], bias=cc.t[:, 0:1], scale=1.0 / n_div)
        act(ss_ap, ss_ap, AF.Exp, [sm.b], [sm.b], scale=-0.5)

    def p0(s, l, pj):
        import os
        LV = int(os.environ.get("P0LVL", "9"))
        src = xT_d[s] if l == 0 else xs_d[s]
        p0st = ExitStack()
        p0ring = Ring(xin + [sb(f"p0x{i}", [128, 512], F32, p0st) for i in range(6)])
        for tb in range(NTB):
            bi = pj.next()
            for j in range(KC):
                xt = p0ring.next()
                dma(QS, xt.t[:], src[j * 128:(j + 1) * 128, tb * 512:(tb + 1) * 512], [], [xt.b])
                if LV >= 2:
                    vts(xb.t[:, j, tb * 512:(tb + 1) * 512], xt.t[:], colp.t[:, C_NG + l * 8 + j:C_NG + l * 8 + j + 1], ALU.mult, [xt.b, colp.b], [xb.buf(tb)])
                if LV >= 3:
                    sq = tmpb_r.next()
                    act(sq.t[:], xt.t[:], AF.Square, [xt.b], [sq.b])
                if LV >= 4:
                    mm(banks[bi][:], ones_bf.t[:], sq.t[:], j == 0, j == KC - 1, [ones_bf.b, sq.b], bank_bufs(bi), inc=True)
            if LV >= 5:
                rstd_from_ps(rstd_bc.t[:, tb * 512:(tb + 1) * 512], banks[bi][:], float(D), bank_bufs(bi), [rstd_bc.buf(tb)])
        if LV >= 6:
            bi = pj.next()
            for t in range(NT):
                mm(banks[bi][:, t:t + 1], rstd_bc.t[:, t * 128:(t + 1) * 128], e0_f.t[:], True, True, [rstd_bc.buf(t // 4), e0_f.b], [qbuf[bi][0]], inc=True)
            vcopy(rstd_col.t[:], banks[bi][:, 0:NT], [qbuf[bi][0]], [rstd_col.b])
        barrier()
        p0st.close()

    def proj_fm(W, c0, tb, bi):
        for kc in range(KC):
            mm(banks[bi][:], W.t[:, kc, c0:c0 + 128], xb.t[:, kc, tb * 512:(tb + 1) * 512], kc == 0, kc == KC - 1,
               [W.b, xb.buf(tb)], bank_bufs(bi), inc=(kc == KC - 1))

    def proj_tm(W, c0, n, t, out_ap, W_bufs):
        for kc in range(KC):
            mm(out_ap, xb.t[:, kc, t * 128:(t + 1) * 128], W.t[:, kc, c0:c0 + n], kc == 0, kc == KC - 1,
               [W.b, xb.buf(t // 4)], W_bufs, inc=(kc == KC - 1))

    def rotary_tables(s):
        with ExitStack() as st:
            posi = sb("posi", [128, S], I32, st)
            tt_ = sb("rt_t", [128, S], F32, st)
            t2 = sb("rt_t2", [128, S], F32, st)
            dma(QS, posi.t[:], pos_d[s:s + 1, :].partition_broadcast(128), [], [posi.b])
            vcopy(tt_.t[:], posi.t[:], [posi.b], [tt_.b])
            vts(tt_.t[:], tt_.t[:], rotc.t[:, 0:1], ALU.mult, [tt_.b, rotc.b], [tt_.b])
            for which in (0, 1):
                if which == 1:
                    vts(tt_.t[:], tt_.t[:], 0.25, ALU.add, [tt_.b], [tt_.b])
                vcopy(posi.t[:], tt_.t[:], [tt_.b], [posi.b])
                vcopy(t2.t[:], posi.t[:], [posi.b], [t2.b])
                vtt(t2.t[:], tt_.t[:], t2.t[:], ALU.subtract, [tt_.b, t2.b], [t2.b])
                act(t2.t[:], t2.t[:], AF.Sin, [t2.b], [t2.b], scale=2.0 * math.pi * (1.0 - 2e-6))
                if which == 0:
                    vts(SIN.t[:], t2.t[:], rotc.t[:, 1:2], ALU.mult, [t2.b, rotc.b], [SIN.b])
                else:
                    vts(COS.t[:], t2.t[:], rotc.t[:, 2:3], ALU.mult, [t2.b, rotc.b], [COS.b], s2=rotc.t[:, 3:4], op1=ALU.add)
            barrier()

    def deltanet_phase(s, l):
        pj = Ring([0, 1])
        Y['c'] = sb("yTc", [128, 4, S], BF16, LY["st"])
        with ExitStack() as st:
            xc = [sb(f"dn_xc{i}", [128, 4 + 512], BF16, st) for i in range(4)]
            dacc_r = Ring([sb(f"dn_acc{i}", [128, 512], F32, st) for i in range(6)])
            dg_r = Ring([sb(f"dn_dg{i}", [128, 128], BF16, st) for i in range(8)])
            pjp = Ring([0, 1, 2, 3, 4, 5, 6, 7])
            ab = sb("dn_ab", [128, NT, 8], F32, st)
            gall = sb("dn_g", [128, NT, 4], F32, st)
            lnb = sb("dn_lnb", [128, NT, 4], F32, st)
            gcgl = sb("dn_gcgl", [128, NT, 8], F32, st)
            ex = sb("dn_ex", [128, 5, NT, 4], F32, st)
            NSTG = 2

            def stage_tiles(i, sl):
                d = {}
                for nm in ("X", "Dl", "Db", "DTb", "decT", "egcb", "N", "NT", "qkT", "qdT", "Rk", "kd", "Rv", "TTa", "TTb",
                           "Pa", "Pb", "PTa", "PTb", "nwT"):
                    d[nm] = sb(f"dn_{nm}{i}_{sl}", [128, 128], BF16, st)
                return d

            qkvT_l = [sb(f"dn_qkvT{i}", [128, 3, S], BF16, st) for i in range(2)]
            zs_l = [sb(f"dn_zs{i}", [128, S], BF16, st) for i in range(2)]
            Sf_l = [sb(f"dn_S{i}", [128, 128], F32, st) for i in range(2)]
            Sb_l = [sb(f"dn_Sb{i}", [128, 128], BF16, st) for i in range(2)]
            stg_l = [[stage_tiles(i, sl) for i in range(NSTG)] for sl in range(2)]
            vn_l = [[sb(f"dn_vn{i}_{sl}", [128, 128], BF16, st) for i in range(2)] for sl in range(2)]
            yc_l = [[sb(f"dn_yc{i}_{sl}", [128, 128], BF16, st) for i in range(2)] for sl in range(2)]

            qring = Ring([(b, 0) for b in range(2, 8)])

            def qt():
                b, q = qring.next()
                return banks[b][:, q * 128:(q + 1) * 128], qbuf[b][q]

            W = wget(("ab", s, l))
            bi = pj.next()
            for t in range(NT):
                proj_tm(W, 0, 8, t, banks[bi][:, t * 8:(t + 1) * 8], [qbuf[bi][0]])
            for t in range(NT):
                vts(ab.t[:, t, :], banks[bi][:, t * 8:(t + 1) * 8], rstd_col.t[:, t:t + 1], ALU.mult, [qbuf[bi][0], rstd_col.b], [ab.b])
            sp = tmpf_r.next()
            spv = sp.t[:, 0:NT * 8].rearrange("p (t c) -> p t c", c=8)
            for h in range(4):
                vts(spv[:, :, h], ab.t[:, :, h], rowb.t[:, R_DTB + l * 4 + h:R_DTB + l * 4 + h + 1], ALU.add, [ab.b, rowb.b], [sp.b])
            vcopy(spv[:, :, 4:8], ab.t[:, :, 4:8], [ab.b], [sp.b])
            act(spv[:, :, 0:4], spv[:, :, 0:4], AF.Exp, [sp.b], [sp.b])
            act(spv[:, :, 4:8], spv[:, :, 4:8], AF.Exp, [sp.b], [sp.b], scale=-1.0)
            act(sp.t[:, 0:NT * 8], sp.t[:, 0:NT * 8], AF.Ln, [sp.b], [sp.b], bias=cc.t[:, 1:2])
            for h in range(4):
                vts(gall.t[:, :, h], spv[:, :, h], negealog.t[:, l * 4 + h:l * 4 + h + 1], ALU.mult, [sp.b, negealog.b], [gall.b])
            vts(lnb.t[:], spv[:, :, 4:8], -1.0, ALU.mult, [sp.b], [lnb.b])
            bi = pj.next()
            for t in range(NT):
                mm(banks[bi][:, t * 8:t * 8 + 4], U_f.t[:], gall.t[:, t, :], True, True, [U_f.b, gall.b], [qbuf[bi][0]], inc=True)
                mm(banks[bi][:, t * 8 + 4:t * 8 + 8], ones_f.t[:], gall.t[:, t, :], True, True, [ones_f.b, gall.b], [qbuf[bi][0]], inc=True)
            vcopy(gcgl.t[:], banks[bi][:, 0:NT * 8].rearrange("p (t c) -> p t c", c=8), [qbuf[bi][0]], [gcgl.b])
            gc = gcgl.t[:, :, 0:4]
            gl = gcgl.t[:, :, 4:8]
            vcopy(ex.t[:, 0], gc, [gcgl.b], [ex.b])
            vtt(ex.t[:, 1], gc, lnb.t[:], ALU.add, [gcgl.b, lnb.b], [ex.b])
            vtt(ex.t[:, 2], gl, gc, ALU.subtract, [gcgl.b], [ex.b])
            vcopy(ex.t[:, 3], lnb.t[:], [lnb.b], [ex.b])
            vcopy(ex.t[:, 4], gl, [gcgl.b], [ex.b])
            act(ex.t[:], ex.t[:], AF.Exp, [ex.b], [ex.b])

            import os
            DLV = int(os.environ.get("DNLVL", "9"))
            Wd = {}
            nblk = {"n": 0}

            def getW(h):
                if h not in Wd:
                    Wd[h] = wget(("dn", s, l, h))
                return Wd[h]

            def qkv_block(h, qkvT, ci, tb, dgs):
                W = getW(h)
                if tb == 0:
                    chunk = (0, 4, 8)[ci] + h
                    for j in range(4):
                        dg_ = dg_r.next()
                        vts(dg_.t[:], ident_bf.t[:], colp.t[:, C_CONV + (l * 4 + j) * 12 + chunk:C_CONV + (l * 4 + j) * 12 + chunk + 1],
                            ALU.mult, [ident_bf.b, colp.b], [dg_.b], E=POOL)
                        dgs.append(dg_)
                n_ = nblk["n"]
                nblk["n"] += 1
                xcur = xc[n_ % 4]
                xprev = xc[(n_ - 1) % 4]
                bi = pjp.next()
                proj_fm(W, ci * 128, tb, bi)
                if tb == 0:
                    issue(DVE, lambda xcur=xcur: nc.vector.memset(xcur.t[:, 0:4], 0.0), [], [xcur.b])
                else:
                    vcopy(xcur.t[:, 0:4], xprev.t[:, 512:516], [xprev.b], [xcur.b], E=POOL)
                vtt(xcur.t[:, 4:516], banks[bi][:], rstd_bc.t[:, tb * 512:(tb + 1) * 512], ALU.mult,
                    bank_bufs(bi) + [rstd_bc.buf(tb)], [xcur.b])
                yield
                b3 = pjp.next()
                for j in range(4):
                    mm(banks[b3][:], dgs[j].t[:], xcur.t[:, 1 + j:1 + j + 512], j == 0, j == 3, [dgs[j].b, xcur.b], bank_bufs(b3), inc=(j == 3))
                dst = qkvT.t[:, ci, tb * 512:(tb + 1) * 512]
                if ci == 2:
                    act(dst, banks[b3][:], AF.Silu, bank_bufs(b3), [qkvT.buf((ci, tb))])
                    return
                acc = dacc_r.next()
                act(acc.t[:], banks[b3][:], AF.Silu, bank_bufs(b3), [acc.b])
                sq = tmpb_r.next()
                act(sq.t[:], acc.t[:], AF.Square, [acc.b], [sq.b])
                yield
                b2 = pjp.next()
                mm(banks[b2][:], ones_bf.t[:], sq.t[:], True, True, [ones_bf.b, sq.b], bank_bufs(b2), inc=True)
                rn = dacc_r.next()
                rstd_from_ps(rn.t[:], banks[b2][:], 1.0, bank_bufs(b2), [rn.b], bias_extra=(cc.t[:, 2:3] if ci == 0 else None), inplace=True)
                yield
                vtt(dst, acc.t[:], rn.t[:], ALU.mult, [acc.b, rn.b], [qkvT.buf((ci, tb))])

            def z_block(h, zs, tb):
                W = getW(h)
                sl_ = slice(tb * 512, (tb + 1) * 512)
                bi = pjp.next()
                proj_fm(W, 384, tb, bi)
                yield
                t1 = dacc_r.next()
                vtt(t1.t[:], banks[bi][:], rstd_bc.t[:, sl_], ALU.mult, bank_bufs(bi) + [rstd_bc.buf(tb)], [t1.b])
                yield
                act(zs.t[:, sl_], t1.t[:], AF.Silu, [t1.b], [zs.buf(tb)])

            def project_blocks(h, qkvT, zs):
                for ci in range(3):
                    dgs = []
                    for tb in range(NTB):
                        yield qkv_block(h, qkvT, ci, tb, dgs)
                for tb in range(NTB):
                    yield z_block(h, zs, tb)

            def make_scan(h, qkvT, zs, stg, vn, yc, Sf, Sb_):
                def col(k, t):
                    return ex.t[:, k, t, h:h + 1]

                def prep(t):
                    T_ = stg[t % NSTG]
                    tok = slice(t * 128, (t + 1) * 128)
                    qT_ = qkvT.t[:, 0, tok]
                    kT_ = qkvT.t[:, 1, tok]
                    vT_ = qkvT.t[:, 2, tok]
                    qb_, kb_, vb_ = qkvT.buf((0, t // 4)), qkvT.buf((1, t // 4)), qkvT.buf((2, t // 4))
                    vts(T_["X"].t[:], U_bf.t[:], gall.t[:, t, h:h + 1], ALU.mult, [U_bf.b, gall.b], [T_["X"].b])
                    yield
                    vts(T_["Dl"].t[:], ident_bf.t[:], lnb.t[:, t, h:h + 1], ALU.mult, [ident_bf.b, lnb.b], [T_["Dl"].b], E=POOL)
                    yield
                    X, Dl = T_["X"], T_["Dl"]
                    p1, b1 = qt()
                    mm(p1, X.t[:], Y_bf.t[:], True, False, [X.b, Y_bf.b], [b1], inc=False)
                    mm(p1, ident_bf.t[:], MnS_bf.t[:], False, True, [ident_bf.b, MnS_bf.b], [b1], inc=True)
                    act(T_["Db"].t[:], p1, AF.Exp, [b1, lnb.b], [T_["Db"].b], bias=lnb.t[:, t, h:h + 1])
                    yield
                    p2, b2 = qt()
                    mm(p2, Y_bf.t[:], X.t[:], True, False, [X.b, Y_bf.b], [b2], inc=False)
                    mm(p2, ones_bf.t[:], Dl.t[:], False, False, [ones_bf.b, Dl.b], [b2], inc=False)
                    mm(p2, ident_bf.t[:], MnST_bf.t[:], False, True, [ident_bf.b, MnST_bf.b], [b2], inc=True)
                    act(T_["DTb"].t[:], p2, AF.Exp, [b2], [T_["DTb"].b])
                    yield
                    p3, b3 = qt()
                    mm(p3, Y_bf.t[:], X.t[:], True, False, [X.b, Y_bf.b], [b3], inc=False)
                    mm(p3, ident_bf.t[:], MnIT_bf.t[:], False, True, [ident_bf.b, MnIT_bf.b], [b3], inc=True)
                    act(T_["decT"].t[:], p3, AF.Exp, [b3], [T_["decT"].b])
                    yield
                    p4, b4 = qt()
                    mm(p4, ones_bf.t[:], X.t[:], True, True, [ones_bf.b, X.b], [b4], inc=True)
                    act(T_["egcb"].t[:], p4, AF.Exp, [b4], [T_["egcb"].b])
                    yield
                    p5, b5 = qt()
                    mm(p5, kT_, kT_, True, True, [kb_], [b5], inc=True)
                    vstt(T_["N"].t[:], p5, -1.0, T_["Db"].t[:], ALU.mult, ALU.mult, [b5, T_["Db"].b], [T_["N"].b])
                    vstt(T_["NT"].t[:], p5, -1.0, T_["DTb"].t[:], ALU.mult, ALU.mult, [b5, T_["DTb"].b], [T_["NT"].b])
                    yield
                    p6, b6 = qt()
                    mm(p6, kT_, qT_, True, True, [kb_, qb_], [b6], inc=True)
                    vtt(T_["qkT"].t[:], p6, T_["decT"].t[:], ALU.mult, [b6, T_["decT"].b], [T_["qkT"].b])
                    yield
                    vtt(T_["qdT"].t[:], qT_, T_["egcb"].t[:], ALU.mult, [qb_, T_["egcb"].b], [T_["qdT"].b], E=POOL)
                    yield
                    p7, b7 = qt()
                    mm(p7, kT_, ident_bf.t[:], True, True, [kb_, ident_bf.b], [b7], inc=True)
                    act(T_["Rk"].t[:], p7, AF.Copy, [b7, ex.b], [T_["Rk"].b], scale=col(1, t))
                    vts(T_["kd"].t[:], p7, col(2, t), ALU.mult, [b7, ex.b], [T_["kd"].b])
                    yield
                    p8, b8 = qt()
                    mm(p8, vT_, ident_bf.t[:], True, True, [vb_, ident_bf.b], [b8], inc=True)
                    act(T_["Rv"].t[:], p8, AF.Copy, [b8, ex.b], [T_["Rv"].b], scale=col(3, t))
                    yield
                    TTc, TTn = T_["TTa"], T_["TTb"]
                    vtt(TTc.t[:], ident_bf.t[:], T_["NT"].t[:], ALU.add, [ident_bf.b, T_["NT"].b], [TTc.b])
                    yield
                    Pc, PTc = T_["N"], T_["NT"]
                    Pn, PTn = T_["Pa"], T_["PTa"]
                    Pn2, PTn2 = T_["Pb"], T_["PTb"]
                    for k in range(1, 7):
                        pp, bp = qt()
                        mm(pp, PTc.t[:], Pc.t[:], True, True, [PTc.b, Pc.b], [bp], inc=True)
                        act(Pn.t[:], pp, AF.Copy, [bp], [Pn.b])
                        yield
                        if k < 6:
                            pq, bq = qt()
                            mm(pq, Pc.t[:], PTc.t[:], True, True, [PTc.b, Pc.b], [bq], inc=True)
                            vcopy(PTn.t[:], pq, [bq], [PTn.b])
                            yield
                        pr, br = qt()
                        mm(pr, Pn.t[:], TTc.t[:], True, True, [Pn.b, TTc.b], [br], inc=True)
                        vtt(TTn.t[:], pr, TTc.t[:], ALU.add, [br, TTc.b], [TTn.b])
                        yield
                        TTc, TTn = TTn, TTc
                        Pc, PTc = Pn, PTn
                        Pn, PTn, Pn2, PTn2 = Pn2, PTn2, Pn, PTn
                    T_["TT"] = TTc
                    pw, bw = qt()
                    mm(pw, T_["Rk"].t[:], TTc.t[:], True, True, [T_["Rk"].b, TTc.b], [bw], inc=True)
                    act(T_["nwT"].t[:], pw, AF.Copy, [bw], [T_["nwT"].b], scale=-1.0)
                    yield

                def chain(t):
                    T_ = stg[t % NSTG]
                    TTc = T_["TT"]
                    v_ = vn[t % 2]
                    y_ = yc[t % 2]
                    pv, bv = qt()
                    mm(pv, TTc.t[:], T_["Rv"].t[:], True, t == 0, [TTc.b, T_["Rv"].b], [bv], inc=(t == 0))
                    if t > 0:
                        mm(pv, T_["nwT"].t[:], Sb_.t[:], False, True, [T_["nwT"].b, Sb_.b], [bv], inc=True)
                    vcopy(v_.t[:], pv, [bv], [v_.b])
                    yield
                    po, bo = qt()
                    if t > 0:
                        mm(po, T_["qdT"].t[:], Sb_.t[:], True, False, [T_["qdT"].b, Sb_.b], [bo], inc=False)
                    mm(po, T_["qkT"].t[:], v_.t[:], t == 0, True, [T_["qkT"].b, v_.b], [bo], inc=True)
                    sm = small_r.next()
                    junk = tmpf_r.next()
                    act(junk.t[:, 0:128], po, AF.Square, [bo], [junk.b, sm.b], accum_out=sm.t[:, 0:1])
                    small_rstd(sm.t[:, 0:1], 128.0, [sm.b], sm)
                    vts(y_.t[:], po, sm.t[:, 0:1], ALU.mult, [bo, sm.b], [y_.b])
                    yield
                    if t < NT - 1:
                        pS, bS = qt()
                        mm(pS, T_["kd"].t[:], v_.t[:], True, True, [T_["kd"].b, v_.b], [bS], inc=True)
                        if t == 0:
                            vcopy(Sf.t[:], pS, [bS], [Sf.b])
                        else:
                            vstt(Sf.t[:], Sf.t[:], col(4, t), pS, ALU.mult, ALU.add, [Sf.b, ex.b, bS], [Sf.b])
                        yield
                        act(Sb_.t[:], Sf.t[:], AF.Copy, [Sf.b], [Sb_.b])
                        yield
                    pt, bt = qt()
                    mm(pt, y_.t[:], ident_bf.t[:], True, True, [y_.b, ident_bf.b], [bt], inc=True)
                    vstt(Y['c'].t[:, h, t * 128:(t + 1) * 128], pt, colp.t[:, C_DNG + l:C_DNG + l + 1], zs.t[:, t * 128:(t + 1) * 128], ALU.mult, ALU.mult,
                         [bt, colp.b, zs.buf(t // 4)], [Y['c'].buf((h, t // 4))])
                    yield

                yield from prep(0)
                for t in range(NT):
                    sub = [chain(t)]
                    if t + 1 < NT:
                        sub.append(prep(t + 1))
                    while sub:
                        for g_ in list(sub):
                            try:
                                next(g_)
                                yield
                            except StopIteration:
                                sub.remove(g_)

            for hp in range(2):
                gens = []
                def pair_blocks(hp=hp):
                    for idx in range(2):
                        yield from project_blocks(2 * hp + idx, qkvT_l[idx], zs_l[idx])
                run_pipelined(pair_blocks(), 3)
                for idx in range(2):
                    h = 2 * hp + idx
                    gens.append(make_scan(h, qkvT_l[idx], zs_l[idx], stg_l[idx], vn_l[idx], yc_l[idx], Sf_l[idx], Sb_l[idx]))
                while gens:
                    for g_ in list(gens):
                        try:
                            next(g_)
                        except StopIteration:
                            gens.remove(g_)
            barrier()

    def attention_phase(s, l):
        li = 0.8 - 0.6 * math.exp(-0.3 * l)
        Y['b'] = sb("yTb", [128, 4, S], BF16, LY["st"])
        with ExitStack() as st:
            QT = sb("at_QT", [128, S], BF16, st)
            KT = sb("at_KT", [128, S], BF16, st)
            Va = sb("at_Va", [128, NT, 130], BF16, st)
            zs = sb("at_zs", [128, NT, 128], BF16, st)
            PTt = [sb(f"at_PT{i}", [128, 2, 512], BF16, st) for i in range(2)]
            osb = [sb(f"at_o{i}", [128, 128], F32, st) for i in range(2)]
            ybt = [sb(f"at_yb{i}", [128, 128], BF16, st) for i in range(3)]
            issue(DVE, lambda: nc.vector.memset(Va.t[:, :, 128:129], 1.0), [], [Va.buf(i) for i in range(NT)])
            pj = Ring([4, 5, 6, 7])
            pjq = Ring([0, 1, 2, 3, 4, 5, 6, 7])
            raw_r = Ring([sb(f"at_raw{i}", [128, 512], BF16, st) for i in range(5)])
            sset = Ring([(4, 5), (6, 7)])
            cnt = {"q": 0}
            import os
            ALV = int(os.environ.get("ATLVL", "9"))
            for h in range(4):
                W = wget(("at", s, l, h))
                def qk_block(ci, dstT, tb, W=W):
                    sl_ = slice(tb * 512, (tb + 1) * 512)
                    bi = pjq.next()
                    proj_fm(W, ci * 128, tb, bi)
                    raw = raw_r.next()
                    vtt(raw.t[:], banks[bi][:], rstd_bc.t[:, sl_], ALU.mult, bank_bufs(bi) + [rstd_bc.buf(tb)], [raw.b])
                    yield
                    b2 = pjq.next()
                    mm(banks[b2][:], Psw_bf.t[:], raw.t[:], True, True, [Psw_bf.b, raw.b], bank_bufs(b2), inc=True)
                    yield
                    t1 = tmpf_r.next()
                    vtt(t1.t[:], raw.t[:], COS.t[:, sl_], ALU.mult, [raw.b, COS.b], [t1.b])
                    t2 = tmpf_r.next()
                    vtt(t2.t[:], banks[b2][:], SIN.t[:, sl_], ALU.mult, bank_bufs(b2) + [SIN.b], [t2.b])
                    vtt(dstT.t[:, sl_], t1.t[:], t2.t[:], ALU.add, [t1.b, t2.b], [dstT.buf(tb)])

                def vz_block(t, W=W):
                    bi = pjq.next()
                    o_ = banks[bi][:, 0:256]
                    hb = bank_bufs(bi)
                    proj_tm(W, 256, 256, t, o_, hb)
                    yield
                    vts(Va.t[:, t, 0:128], o_[:, 0:128], rstd_col.t[:, t:t + 1], ALU.mult, hb + [rstd_col.b], [Va.buf(t)])
                    act(zs.t[:, t, :], o_[:, 128:256], AF.Silu, hb + [rstd_col.b], [zs.buf(t)], scale=rstd_col.t[:, t:t + 1])
                    yield
                    vstt(zs.t[:, t, :], zs.t[:, t, :], 1.0 - li, rowb.t[:, R_SUBLN + l * 128:R_SUBLN + (l + 1) * 128], ALU.mult, ALU.mult,
                         [zs.buf(t), rowb.b], [zs.buf(t)])

                def at_blocks():
                    for ci, dstT in ((0, QT), (1, KT)):
                        for tb in range(NTB):
                            yield qk_block(ci, dstT, tb)
                    for t in range(NT):
                        yield vz_block(t)
                run_pipelined(at_blocks(), 3)
                iters = [(qb, kt) for qb in range(NTB) for kt in range(4 * qb + 4)]
                state = {}

                def emit_S(it):
                    qb, kt = it
                    j0 = max(0, kt - 4 * qb)
                    n = 512 - j0 * 128
                    qs = qb * 512 + j0 * 128
                    sb_ = sset.next()
                    state[it] = sb_
                    for c in range(2):
                        mm(banks[sb_[c]][:, 0:n], KT.t[c * 64:(c + 1) * 64, kt * 128:(kt + 1) * 128], QT.t[c * 64:(c + 1) * 64, qs:qs + n],
                           True, True, [KT.buf(kt // 4), QT.buf(qb)], bank_bufs(sb_[c]), inc=True)

                def emit_exp(it):
                    qb, kt = it
                    j0 = max(0, kt - 4 * qb)
                    n = 512 - j0 * 128
                    sb_ = state.pop(it)
                    P_ = PTt[cnt["q"] % 2]
                    cnt["q"] += 1
                    for c in range(2):
                        act(P_.t[:, c, 0:n], banks[sb_[c]][:, 0:n], AF.Exp, bank_bufs(sb_[c]), [P_.b], scale=0.125)
                    if kt >= 4 * qb:
                        for c in range(2):
                            vtt(P_.t[:, c, 0:128], P_.t[:, c, 0:128], U_bf.t[:], ALU.mult, [P_.b, U_bf.b], [P_.b])
                    return P_

                def emit_AV(it, P_):
                    qb, kt = it
                    j0 = max(0, kt - 4 * qb)
                    while pending:
                        pending.pop(0)()
                    for j in range(j0, 4):
                        tq = 4 * qb + j
                        for c in range(2):
                            mm(banks[j][:, c * 129:(c + 1) * 129], P_.t[:, c, (j - j0) * 128:(j - j0 + 1) * 128], Va.t[:, kt, 0:129],
                               (kt == 0 and c == 0), kt == tq, [P_.b, Va.buf(kt)], [qbuf[j][0]], inc=(c == 1))
                        if kt == tq:
                            ob = [qbuf[j][0]]
                            sm = small_r.next()
                            issue(DVE, lambda j=j, sm=sm: nc.vector.reciprocal(out=sm.t[:, 0:1], in_=banks[j][:, 128:129]), ob, [sm.b])
                            issue(DVE, lambda j=j, sm=sm: nc.vector.reciprocal(out=sm.t[:, 1:2], in_=banks[j][:, 257:258]), ob, [sm.b])
                            vtt(sm.t[:, 1:2], sm.t[:, 1:2], lam_neg.t[:, l:l + 1], ALU.mult, [sm.b, lam_neg.b], [sm.b])
                            o_ = osb[tq % 2]
                            vts(o_.t[:], banks[j][:, 0:128], sm.t[:, 0:1], ALU.mult, ob + [sm.b], [o_.b])
                            vstt(o_.t[:], banks[j][:, 129:257], sm.t[:, 1:2], o_.t[:], ALU.mult, ALU.add, ob + [sm.b, o_.b], [o_.b])
                            junk = tmpf_r.next()
                            act(junk.t[:, 0:128], o_.t[:], AF.Square, [o_.b], [junk.b, sm.b], accum_out=sm.t[:, 2:3])
                            small_rstd(sm.t[:, 2:3], 128.0, [sm.b], sm)
                            y_ = ybt[tq % 3]
                            vstt(y_.t[:], o_.t[:], sm.t[:, 2:3], zs.t[:, tq, :], ALU.mult, ALU.mult, [o_.b, sm.b, zs.buf(tq)], [y_.b])
                            def fin(j=j, y_=y_, tq=tq):
                                mm(banks[j][:, 384:512], y_.t[:], ident_bf.t[:], True, True, [y_.b, ident_bf.b], [qbuf[j][3]], inc=True)
                                act(Y['b'].t[:, h, tq * 128:(tq + 1) * 128], banks[j][:, 384:512], AF.Copy, [qbuf[j][3]], [Y['b'].buf((h, tq // 4))])
                            pending.append(fin)

                pending = []
                emit_S(iters[0])
                for i_, it in enumerate(iters):
                    P_ = emit_exp(it)
                    if i_ + 1 < len(iters):
                        emit_S(iters[i_ + 1])
                    emit_AV(it, P_)
                while pending:
                    pending.pop(0)()
            barrier()

    def gmlp_phase(s, l):
        pj = Ring([0, 1, 2, 3])
        Y['a'] = sb("yTa", [128, 4, S], BF16, LY["st"])
        with ExitStack() as st:
            wtmp = sb("gm_wtmp", [128, 4, 128], F32, st)
            bsb = sb("gm_bsb", [128, 512], F32, st)
            dma(QS, wtmp.t[:], wsT_d[l].rearrange("g s t -> s g t"), [], [wtmp.b])
            dma(QS, bsb.t[:], bs_d[l:l + 1, :].partition_broadcast(128), [], [bsb.b])
            for gi in range(4):
                issue(POOL, lambda gi=gi: g.affine_select(out=WTg.t[:, gi, :], in_=wtmp.t[:, gi, :], pattern=[[1, 128]], compare_op=ALU.is_ge,
                                                          fill=0.0, base=0, channel_multiplier=-1), [wtmp.b], [WTg.b])
            bi = pj.next()
            for gi in range(4):
                mm(banks[bi][:, gi * 128:(gi + 1) * 128], ones_bf.t[:], WTg.t[:, gi, :], True, True, [ones_bf.b, WTg.b], [qbuf[bi][gi]], inc=True)
            for gi in range(4):
                vstt(Et.t[:, gi, :], banks[bi][:, gi * 128:(gi + 1) * 128], colp.t[:, C_LNB + l * 4 + gi:C_LNB + l * 4 + gi + 1],
                     bsb.t[:, gi * 128:(gi + 1) * 128], ALU.mult, ALU.add, [qbuf[bi][gi], colp.b, bsb.b], [Et.b])
            def uz_block(W, nm, fn, gi, tb):
                sl_ = slice(tb * 512, (tb + 1) * 512)
                bi = pj8.next()
                proj_fm(W, gi * 128, tb, bi)
                yield
                t1 = tmpf_r.next()
                vtt(t1.t[:], banks[bi][:], rstd_bc.t[:, sl_], ALU.mult, bank_bufs(bi) + [rstd_bc.buf(tb)], [t1.b])
                yield
                if nm == "gu":
                    act(Y['a'].t[:, gi, sl_], t1.t[:], fn, [t1.b], [Y['a'].buf((gi, tb))])
                else:
                    t2 = tmpb_r.next()
                    act(t2.t[:], t1.t[:], fn, [t1.b], [t2.b])
                    yield
                    vtt(Y['a'].t[:, gi, sl_], Y['a'].t[:, gi, sl_], t2.t[:], ALU.mult, [Y['a'].buf((gi, tb)), t2.b], [Y['a'].buf((gi, tb))], E=POOL)

            pj8 = Ring([0, 1, 2, 3, 4, 5, 6, 7])
            for nm, fn in (("gu", AF.Gelu_apprx_tanh), ("gz", AF.Silu)):
                W = wget((nm, s, l))
                run_pipelined((uz_block(W, nm, fn, gi, tb) for gi in range(4) for tb in range(NTB)), 3)
            W = wget(("gv", s, l))

            def v_block(t):
                tok = slice(t * 128, (t + 1) * 128)
                bi = pj8.next()
                proj_tm(W, 0, 512, t, banks[bi][:], bank_bufs(bi))
                yield
                v = tmpf_r.next()
                act(v.t[:], banks[bi][:], AF.Gelu_apprx_tanh, bank_bufs(bi) + [rstd_col.b], [v.b], scale=rstd_col.t[:, t:t + 1])
                yield
                sm = small_r.next()
                issue(DVE, lambda v=v, sm=sm: nc.vector.bn_stats(out=sm.t[:, 0:6], in_=v.t[:]), [v.b], [sm.b])
                issue(DVE, lambda sm=sm: nc.vector.bn_aggr(out=sm.t[:, 6:8], in_=sm.t[:, 0:6]), [sm.b], [sm.b])
                yield
                small_rstd(sm.t[:, 7:8], 1.0, [sm.b], sm)
                yield
                vb = tmpb_r.next()
                vts(vb.t[:], v.t[:], sm.t[:, 6:7], ALU.subtract, [v.b, sm.b], [vb.b], s2=sm.t[:, 7:8], op1=ALU.mult)
                yield
                b2 = pj8.next()
                for gi in range(4):
                    mm(banks[b2][:, gi * 128:(gi + 1) * 128], vb.t[:, gi * 128:(gi + 1) * 128], WTg.t[:, gi, :], True, True,
                       [vb.b, WTg.b], bank_bufs(b2), inc=True)
                yield
                mx = tmpf_r.next()
                for gi in range(4):
                    vstt(mx.t[:, gi * 128:(gi + 1) * 128], banks[b2][:, gi * 128:(gi + 1) * 128], colp.t[:, C_LNG + l * 4 + gi:C_LNG + l * 4 + gi + 1], Et.t[:, gi, :],
                         ALU.mult, ALU.add, bank_bufs(b2) + [colp.b, Et.b], [mx.b])
                yield
                for gi in range(4):
                    vtt(Y['a'].t[:, gi, tok], Y['a'].t[:, gi, tok], mx.t[:, gi * 128:(gi + 1) * 128], ALU.mult, [Y['a'].buf((gi, t // 4)), mx.b], [Y['a'].buf((gi, t // 4))], E=POOL)

            run_pipelined((v_block(t) for t in range(NT)), 2)
            barrier()

    def merge_tail_phase(s, l):
        pj = Ring([0, 1, 2, 3])
        last = (l == nl - 1)
        with ExitStack() as st:
            mg = sb("mg", [128, KC, S], BF16, st)
            x3 = [sb(f"x3_{i}", [128, 512], F32, st) for i in range(3)]
            x3_r = Ring(xin + x3)
            PD = 2

            class Prefetch:
                def __init__(self, order, srcfn, depfn):
                    self.order, self.srcfn, self.depfn = order, srcfn, depfn
                    self.nxt = 0
                    self.tiles = {}

                def get(self, i):
                    while self.nxt <= min(i + PD, len(self.order) - 1):
                        j_, tb_ = self.order[self.nxt]
                        xt_ = x3_r.next()
                        dma(QS, xt_.t[:], self.srcfn(j_, tb_), self.depfn(j_, tb_), [xt_.b])
                        self.tiles[self.nxt] = xt_
                        self.nxt += 1
                    return self.tiles.pop(i)
            pTb = sb("pTb", [128, 2, S], BF16, st)
            rstd2 = rstd_bc
            dma(QW, pTb.t[:], pT_d[l, s].rearrange("(k p) t -> p k t", p=128), [], [pTb.b])
            for j in range(8):
                Wg = wget(("mg", s, l, j))
                Wb = wget(("mb", s, l, j))
                for tb in range(NTB):
                    sl_ = slice(tb * 512, (tb + 1) * 512)
                    acc = None
                    for br in range(3):
                        bg = pj.next()
                        proj_fm(Wg, br * 128, tb, bg)
                        gt = tmpf_r.next()
                        vtt(gt.t[:], banks[bg][:], rstd_bc.t[:, sl_], ALU.mult, bank_bufs(bg) + [rstd_bc.buf(tb)], [gt.b])
                        act(gt.t[:], gt.t[:], AF.Sigmoid, [gt.b], [gt.b])
                        by = pj.next()
                        for kc in range(4):
                            mm(banks[by][:], Wb.t[:, kc, br * 128:(br + 1) * 128], Y['abc'[br]].t[:, kc, sl_], kc == 0, kc == 3,
                               [Wb.b, Y['abc'[br]].buf((kc, tb))], bank_bufs(by), inc=(kc == 3))
                        if br == 0:
                            acc = tmpf_r.next()
                            vtt(acc.t[:], banks[by][:], gt.t[:], ALU.mult, bank_bufs(by) + [gt.b], [acc.b])
                        else:
                            vtt(gt.t[:], banks[by][:], gt.t[:], ALU.mult, bank_bufs(by) + [gt.b], [gt.b])
                            if br == 1:
                                vtt(acc.t[:], acc.t[:], gt.t[:], ALU.add, [acc.b, gt.b], [acc.b], E=POOL)
                            else:
                                vtt(mg.t[:, j, sl_], acc.t[:], gt.t[:], ALU.add, [acc.b, gt.b], [mg.buf(tb)], E=POOL)
            src = xT_d[s] if l == 0 else xs_d[s]
            xsb = [[Buf(f"xs{j}_{tb}") for tb in range(NTB)] for j in range(8)]
            order = [(j, tb) for j in range(8) for tb in range(NTB)]
            pf1 = Prefetch(order, lambda j_, tb_: src[j_ * 128:(j_ + 1) * 128, tb_ * 512:(tb_ + 1) * 512], lambda j_, tb_: [])
            def t1_block(W, j, tb):
                sl_ = slice(tb * 512, (tb + 1) * 512)
                bi = pj.next()
                for kc in range(KC):
                    mm(banks[bi][:], W.t[:, kc, 0:128], mg.t[:, kc, sl_], kc == 0, kc == KC - 1, [W.b, mg.buf(tb)], bank_bufs(bi), inc=(kc == KC - 1))
                xt = pf1.get(j * NTB + tb)
                yield
                vtt(xt.t[:], banks[bi][:], xt.t[:], ALU.add, bank_bufs(bi) + [xt.b], [xt.b])
                yield
                dma(QS, xs_d[s][j * 128:(j + 1) * 128, sl_], xt.t[:], [xt.b], [xsb[j][tb]])
                act(xb.t[:, j, sl_], xt.t[:], AF.Copy, [xt.b, colp.b], [xb.buf(tb)], scale=colp.t[:, C_PG + l * 8 + j:C_PG + l * 8 + j + 1])
                sq = tmpb_r.next()
                act(sq.t[:], xt.t[:], AF.Square, [xt.b], [sq.b])
                yield
                mm(banks[4 + tb][:], ones_bf.t[:], sq.t[:], j == 0, j == 7, [ones_bf.b, sq.b], bank_bufs(4 + tb), inc=True)

            for j in range(8):
                W = wget(("wo", s, l, j))
                run_pipelined((t1_block(W, j, tb) for tb in range(NTB)), 3)
            for tb in range(NTB):
                rstd_from_ps(rstd2.t[:, tb * 512:(tb + 1) * 512], banks[4 + tb][:], float(D), bank_bufs(4 + tb), [rstd2.buf(tb)])
            if debug and s == 0 and l == 0:
                for bi_, nm_ in enumerate('abc'):
                    dma(QS, dbg_d[:, bi_ * 4:(bi_ + 1) * 4, :], Y[nm_].t[:], [Y[nm_].buf((c, tb)) for c in range(4) for tb in range(NTB)], [])
                dma(QS, dbgx_d[0], xs_d[0], [xsb[j][tb] for j in range(8) for tb in range(NTB)], [])
            pf3 = Prefetch(order, lambda j_, tb_: xs_d[s][j_ * 128:(j_ + 1) * 128, tb_ * 512:(tb_ + 1) * 512], lambda j_, tb_: [xsb[j_][tb_]])
            pj3 = Ring([0, 1, 2, 3, 4, 5, 6, 7])

            def t3_block(W, j, tb):
                sl_ = slice(tb * 512, (tb + 1) * 512)
                bg = pj3.next()
                proj_fm(W, 0, tb, bg)
                xt = pf3.get(j * NTB + tb)
                yield
                gt = tmpf_r.next()
                vtt(gt.t[:], banks[bg][:], rstd2.t[:, sl_], ALU.mult, bank_bufs(bg) + [rstd2.buf(tb)], [gt.b])
                yield
                act(gt.t[:], gt.t[:], AF.Sigmoid, [gt.b], [gt.b])
                bp = pj3.next()
                for kc in range(2):
                    mm(banks[bp][:], W.t[:, kc, 128:256], pTb.t[:, kc, sl_], kc == 0, kc == 1, [W.b, pTb.b], bank_bufs(bp), inc=(kc == 1))
                yield
                vtt(gt.t[:], banks[bp][:], gt.t[:], ALU.mult, bank_bufs(bp) + [gt.b], [gt.b])
                yield
                vtt(xt.t[:], xt.t[:], gt.t[:], ALU.add, [xt.b, gt.b], [xt.b], E=POOL)
                yield
                dma(QS, xs_d[s][j * 128:(j + 1) * 128, sl_], xt.t[:], [xt.b], [xsb[j][tb]])

            for j in range(8):
                W = wget(("pg", s, l, j))
                run_pipelined((t3_block(W, j, tb) for tb in range(NTB)), 3)
            if debug and s == 0 and l == 0:
                dma(QS, dbgx_d[1], xs_d[0], [xsb[j][tb] for j in range(8) for tb in range(NTB)], [])
            barrier()
            if last:
                order2 = [(j, tb) for tb in range(NTB) for j in range(KC)]
                pfa = Prefetch(order2, lambda j_, tb_: xs_d[s][j_ * 128:(j_ + 1) * 128, tb_ * 512:(tb_ + 1) * 512], lambda j_, tb_: [xsb[j_][tb_]])
                for tb in range(NTB):
                    bi = 4 + tb
                    for j in range(KC):
                        xt = pfa.get(tb * KC + j)
                        sq = tmpb_r.next()
                        act(sq.t[:], xt.t[:], AF.Square, [xt.b], [sq.b])
                        mm(banks[bi][:], ones_bf.t[:], sq.t[:], j == 0, j == KC - 1, [ones_bf.b, sq.b], bank_bufs(bi), inc=True)
                    rstd_from_ps(rstd2.t[:, tb * 512:(tb + 1) * 512], banks[bi][:], float(D), bank_bufs(bi), [rstd2.buf(tb)])
                pfb = Prefetch(order2, lambda j_, tb_: xs_d[s][j_ * 128:(j_ + 1) * 128, tb_ * 512:(tb_ + 1) * 512], lambda j_, tb_: [xsb[j_][tb_]])
                for tb in range(NTB):
                    sl_ = slice(tb * 512, (tb + 1) * 512)
                    for j in range(KC):
                        xt = pfb.get(tb * KC + j)
                        vstt(xt.t[:], xt.t[:], colp.t[:, C_FG + j:C_FG + j + 1], rstd2.t[:, sl_], ALU.mult, ALU.mult, [xt.b, colp.b, rstd2.buf(tb)], [xt.b])
                        OUT_TOKS.append(dma(QS, yT_d[s][j * 128:(j + 1) * 128, sl_], xt.t[:], [xt.b], []))
                barrier()

    OUT_TOKS = []
    for s in range(nseq):
        if upto >= 1:
            rotary_tables(s)
        for l in range(nl):
            LY["st"] = ExitStack()
            if upto >= 2:
                p0(s, l, Ring([0, 1, 2, 3]))
            if upto >= 3:
                deltanet_phase(s, l)
            if upto >= 4:
                attention_phase(s, l)
            if upto >= 5:
                gmlp_phase(s, l)
            if upto >= 6:
                merge_tail_phase(s, l)
            barrier()
            LY["st"].close()
    if upto < 6 and debug:
        pass
    for sl in QS.slots:
        if sl.count > 0:
            SP.wait((sl.sem, sl.count, sl))
    for sl in QW.slots:
        if sl.count > 0:
            POOL.wait((sl.sem, sl.count, sl))
    for e in (PE, ACT, DVE, POOL):
        if e.count > 0:
            SP.wait(e.cur_tok())
    es.close()
    return nc


def make_in_maps(inputs, ncores=8):
    f = lambda a: np.ascontiguousarray(np.asarray(a, dtype=np.float32))
    x = f(inputs["x"])
    p = f(inputs["p"])
    pos = np.ascontiguousarray(np.asarray(inputs["positions"], dtype=np.int32))
    colp = np.zeros((128, NCOL), np.float32)
    colp[:, C_NG:C_NG + 32] = f(inputs["norm_g"]).reshape(L, 8, 128).transpose(2, 0, 1).reshape(128, 32)
    colp[:, C_PG:C_PG + 32] = f(inputs["ple_norm_g"]).reshape(L, 8, 128).transpose(2, 0, 1).reshape(128, 32)
    colp[:, C_FG:C_FG + 8] = f(inputs["final_norm_g"]).reshape(8, 128).T
    colp[:, C_LNG:C_LNG + 16] = f(inputs["gm_ln_g"]).reshape(L, 4, 128).transpose(2, 0, 1).reshape(128, 16)
    colp[:, C_LNB:C_LNB + 16] = f(inputs["gm_ln_b"]).reshape(L, 4, 128).transpose(2, 0, 1).reshape(128, 16)
    colp[:, C_DNG:C_DNG + 4] = f(inputs["dn_norm_g"]).T
    colp[:, C_CONV:C_CONV + 192] = f(inputs["dn_conv_w"]).reshape(L, 4, 12, 128).transpose(3, 0, 1, 2).reshape(128, 192)
    rowp = np.zeros((1, NROW), np.float32)
    rowp[0, R_SUBLN:R_SUBLN + 512] = f(inputs["da_subln_g"]).reshape(-1)
    rowp[0, R_DNG:R_DNG + 512] = f(inputs["dn_norm_g"]).reshape(-1)
    rowp[0, R_ALOG:R_ALOG + 16] = f(inputs["dn_a_log"]).reshape(-1)
    rowp[0, R_DTB:R_DTB + 16] = f(inputs["dn_dt_bias"]).reshape(-1)
    lamp = np.stack([f(inputs[k]).reshape(-1) for k in ("da_lq1", "da_lk1", "da_lq2", "da_lk2")], 0)
    w_br = np.ascontiguousarray(np.stack([f(inputs["w_br_a"]), f(inputs["w_br_b"]), f(inputs["w_br_c"])], 1))
    shared = {
        "w_in": f(inputs["w_in"]), "w_br": w_br, "w_out": f(inputs["w_out"]), "w_pg": f(inputs["w_ple_gate"]),
        "w_pp": f(inputs["w_ple_proj"]), "gm_wsT": np.ascontiguousarray(f(inputs["gm_ws"]).transpose(0, 1, 3, 2)),
        "gm_bs": f(inputs["gm_bs"]).reshape(L, 512), "gm_lnb": f(inputs["gm_ln_b"]).reshape(L, 512),
        "colp": colp, "rowp": rowp, "lamp": np.ascontiguousarray(lamp),
    }
    maps = []
    for c in range(ncores):
        b0 = c * NSEQ
        m = dict(shared)
        m["xT"] = np.ascontiguousarray(x[b0:b0 + NSEQ].transpose(0, 2, 1))
        m["pT"] = np.ascontiguousarray(p[:, b0:b0 + NSEQ].transpose(0, 1, 3, 2))
        m["pos"] = np.ascontiguousarray(pos[b0:b0 + NSEQ])
        maps.append(m)
    return maps


def kernel(**inputs):
    nc = build()
    maps = make_in_maps(inputs, 8)
    res = run_bass_kernel_spmd(nc, maps, core_ids=list(range(8)))
    outs = [np.asarray(r["yT"]).transpose(0, 2, 1) for r in res.results]
    return np.ascontiguousarray(np.concatenate(outs, axis=0).astype(np.float32))
```
